# Optimizing a Trainium2 kernel written in Bass

```python
import math
import jax, jax.numpy as jnp
from jax import lax
import numpy as np

D_MODEL = 1024
BATCH = 4
SEQ = 8192
DEPTH = 4

N_META = 16
MLA_HEADS = 8
Q_LORA = 384
KV_LORA = 256
QK_NOPE = 64
QK_ROPE = 32
V_HEAD = 64
ROPE_THETA = 10000.0
Q_BLOCK = 128
HG_HEADS = 4
HG_KDIM = 128
HG_VDIM = 128
HG_CHUNK = 64
D_FF = 2816
EPS = 1e-6
NEG_BIG = -1e30
F_MIN = 1e-20

MLA_WIDTH = MLA_HEADS * V_HEAD
HG_FWIDTH = HG_HEADS * HG_KDIM
HG_WIDTH = HG_HEADS * HG_VDIM
IN_SPLITS = (Q_LORA, KV_LORA, QK_ROPE, HG_FWIDTH, HG_FWIDTH, HG_WIDTH, HG_WIDTH, D_MODEL, D_MODEL)
D_IN = sum(IN_SPLITS)

kernel_name = "hybrid_mla_hgrn2_macaron_meta"


def rms_norm(x, w):
    xf = x.astype(jnp.float32)
    y = xf * lax.rsqrt(jnp.mean(xf * xf, axis=-1, keepdims=True) + EPS)
    return (y * w.astype(jnp.float32)).astype(x.dtype)


def split_cols(z, sizes):
    outs, start = [], 0
    for s in sizes:
        outs.append(z[..., start:start + s])
        start += s
    return outs


def swiglu(x, w_gu, w_down):
    gate, up = jnp.split(x @ w_gu, 2, axis=-1)
    return (jax.nn.silu(gate) * up) @ w_down


def rope(x, pos):
    half = x.shape[-1] // 2
    inv = ROPE_THETA ** (-jnp.arange(half, dtype=jnp.float32) / half)
    ang = pos.astype(jnp.float32)[:, None] * inv[None, :]
    cos = jnp.cos(ang)[:, None, :]
    sin = jnp.sin(ang)[:, None, :]
    x1 = x[..., :half].astype(jnp.float32)
    x2 = x[..., half:].astype(jnp.float32)
    return jnp.concatenate([x1 * cos - x2 * sin, x2 * cos + x1 * sin], axis=-1).astype(x.dtype)


def mla(c_q, c_kv, k_pe, pos, q_norm_w, kv_norm_w, w_uq, w_ukv):
    B, L, _ = c_q.shape
    H, DQK = MLA_HEADS, QK_NOPE + QK_ROPE
    q = (rms_norm(c_q, q_norm_w) @ w_uq).reshape(B, L, H, DQK)
    q = jnp.concatenate([q[..., :QK_NOPE], rope(q[..., QK_NOPE:], pos)], axis=-1)
    kv = (rms_norm(c_kv, kv_norm_w) @ w_ukv).reshape(B, L, H, QK_NOPE + V_HEAD)
    v = kv[..., QK_NOPE:]
    k_rot = rope(k_pe[:, :, None, :], pos)
    k = jnp.concatenate([kv[..., :QK_NOPE], jnp.broadcast_to(k_rot, (B, L, H, QK_ROPE))], axis=-1)
    scale = DQK ** -0.5
    n_blocks = -(-L // Q_BLOCK)
    pad = n_blocks * Q_BLOCK - L
    qb = jnp.pad(q, ((0, 0), (0, pad), (0, 0), (0, 0)))
    qb = qb.reshape(B, n_blocks, Q_BLOCK, H, DQK).transpose(1, 0, 2, 3, 4)
    k_pos = jnp.arange(L)

    def block(args):
        qi, blk = args
        s = jnp.einsum('bqhd,bkhd->bhqk', qi, k).astype(jnp.float32) * scale
        q_pos = blk * Q_BLOCK + jnp.arange(Q_BLOCK)
        s = jnp.where(k_pos[None, :] <= q_pos[:, None], s, NEG_BIG)
        p = jax.nn.softmax(s, axis=-1).astype(v.dtype)
        return jnp.einsum('bhqk,bkhd->bqhd', p, v)

    o = lax.map(block, (qb, jnp.arange(n_blocks)))
    o = o.transpose(1, 0, 2, 3, 4).reshape(B, n_blocks * Q_BLOCK, H * V_HEAD)
    return o[:, :L]


def hgrn2(q_in, f_in, i_in, g_in, lb, norm_w):
    B, L, _ = q_in.shape
    dt = q_in.dtype
    f32 = jnp.float32
    q = jax.nn.silu(q_in.astype(f32)).reshape(B, L, HG_HEADS, HG_KDIM)
    z = f_in.astype(f32).reshape(B, L, HG_HEADS, HG_KDIM)
    lbf = lb.astype(f32).reshape(HG_HEADS, HG_KDIM)
    f = lbf + (1.0 - lbf) * jax.nn.sigmoid(z)
    log_f = jnp.log(jnp.maximum(f, F_MIN))
    k = (1.0 - lbf) * jax.nn.sigmoid(-z)
    v = i_in.astype(f32).reshape(B, L, HG_HEADS, HG_VDIM)
    front = (-N_META) % HG_CHUNK

    def to_chunks(t):
        t = jnp.pad(t, ((0, 0), (front, 0), (0, 0), (0, 0)))
        n = t.shape[1] // HG_CHUNK
        return t.reshape(B, n, HG_CHUNK, HG_HEADS, t.shape[-1]).transpose(1, 0, 3, 2, 4)

    qc, kc, vc, gc = to_chunks(q), to_chunks(k), to_chunks(v), to_chunks(log_f)
    scale = HG_KDIM ** -0.5
    causal = jnp.tril(jnp.ones((HG_CHUNK, HG_CHUNK), dtype=bool))

    def step(S, inp):
        qt, kt, vt, gt = inp
        b = jnp.cumsum(gt, axis=2)
        diff = b[:, :, :, None, :] - b[:, :, None, :, :]
        decay = jnp.exp(jnp.where(causal[:, :, None], diff, NEG_BIG))
        attn = jnp.einsum('bhtk,bhtsk,bhsk->bhts', qt, decay, kt) * scale
        o = jnp.einsum('bhts,bhsv->bhtv', attn, vt) + \
            jnp.einsum('bhtk,bhkv->bhtv', qt * jnp.exp(b) * scale, S)
        b_last = b[:, :, -1:, :]
        S = jnp.exp(b_last[:, :, 0, :])[..., None] * S + \
            jnp.einsum('bhsk,bhsv->bhkv', kt * jnp.exp(b_last - b), vt)
        return S, o

    S0 = jnp.zeros((B, HG_HEADS, HG_KDIM, HG_VDIM), f32)
    _, o = lax.scan(step, S0, (qc, kc, vc, gc))
    n = o.shape[0]
    o = o.transpose(1, 0, 3, 2, 4).reshape(B, n * HG_CHUNK, HG_HEADS, HG_VDIM)[:, front:]
    o = o * lax.rsqrt(jnp.mean(o * o, axis=-1, keepdims=True) + EPS) * norm_w.astype(f32)
    gate = jax.nn.silu(g_in.astype(f32)).reshape(B, L, HG_HEADS, HG_VDIM)
    return (o * gate).reshape(B, L, HG_WIDTH).astype(dt)


def setup_inputs(seed: int = 0) -> dict:
    key = jax.random.key(seed)
    ks = jax.random.split(key, 24)
    f32 = jnp.float32

    def nrm(k, shape, fan_in):
        return jax.random.normal(k, shape, f32) * (fan_in ** -0.5)

    def gain(k, shape):
        return 1.0 + 0.02 * jax.random.normal(k, shape, f32)

    return {
        "x": jax.random.normal(ks[0], (BATCH, SEQ, D_MODEL), f32),
        "meta_tokens": jax.random.normal(ks[1], (N_META, D_MODEL), f32),
        "ffn1_norm": gain(ks[2], (DEPTH, D_MODEL)),
        "ffn1_w_gu": nrm(ks[3], (DEPTH, D_MODEL, 2 * D_FF), D_MODEL),
        "ffn1_w_down": nrm(ks[4], (DEPTH, D_FF, D_MODEL), D_FF),
        "mix_norm": gain(ks[5], (DEPTH, D_MODEL)),
        "w_in": nrm(ks[6], (DEPTH, D_MODEL, D_IN), D_MODEL),
        "q_norm": gain(ks[7], (DEPTH, Q_LORA)),
        "kv_norm": gain(ks[8], (DEPTH, KV_LORA)),
        "w_uq": nrm(ks[9], (DEPTH, Q_LORA, MLA_HEADS * (QK_NOPE + QK_ROPE)), Q_LORA),
        "w_ukv": nrm(ks[10], (DEPTH, KV_LORA, MLA_HEADS * (QK_NOPE + V_HEAD)), KV_LORA),
        "hg_lb_raw": 0.5 * jax.random.normal(ks[11], (DEPTH, HG_FWIDTH), f32),
        "hg_norm": gain(ks[12], (DEPTH, HG_VDIM)),
        "w_proj_attn": nrm(ks[13], (DEPTH, MLA_WIDTH, D_MODEL), MLA_WIDTH),
        "w_proj_rec": nrm(ks[14], (DEPTH, HG_WIDTH, D_MODEL), HG_WIDTH),
        "w_out": nrm(ks[15], (DEPTH, D_MODEL, D_MODEL), D_MODEL),
        "ffn2_norm": gain(ks[16], (DEPTH, D_MODEL)),
        "ffn2_w_gu": nrm(ks[17], (DEPTH, D_MODEL, 2 * D_FF), D_MODEL),
        "ffn2_w_down": nrm(ks[18], (DEPTH, D_FF, D_MODEL), D_FF),
        "final_norm": gain(ks[19], (D_MODEL,)),
    }


def reference(x, meta_tokens, ffn1_norm, ffn1_w_gu, ffn1_w_down, mix_norm, w_in, q_norm, kv_norm,
              w_uq, w_ukv, hg_lb_raw, hg_norm, w_proj_attn, w_proj_rec, w_out,
              ffn2_norm, ffn2_w_gu, ffn2_w_down, final_norm):
    B = x.shape[0]
    meta = jnp.broadcast_to(meta_tokens.astype(x.dtype)[None], (B, N_META, D_MODEL))
    h = jnp.concatenate([meta, x], axis=1)
    L = h.shape[1]
    pos = jnp.arange(L)
    p_lb = jax.nn.softmax(hg_lb_raw.astype(jnp.float32), axis=0)
    lbs = jnp.cumsum(p_lb, axis=0) - p_lb[0:1]
    for l in range(DEPTH):
        h = h + 0.5 * swiglu(rms_norm(h, ffn1_norm[l]), ffn1_w_gu[l], ffn1_w_down[l])
        u = rms_norm(h, mix_norm[l])
        c_q, c_kv, k_pe, hq, hf, hi, hg, ga, gb = split_cols(u @ w_in[l], IN_SPLITS)
        y_a = mla(c_q, c_kv, k_pe, pos, q_norm[l], kv_norm[l], w_uq[l], w_ukv[l]) @ w_proj_attn[l]
        y_b = hgrn2(hq, hf, hi, hg, lbs[l], hg_norm[l]) @ w_proj_rec[l]
        merged = jax.nn.sigmoid(ga) * y_a + jax.nn.sigmoid(gb) * y_b
        h = h + merged @ w_out[l]
        h = h + 0.5 * swiglu(rms_norm(h, ffn2_norm[l]), ffn2_w_gu[l], ffn2_w_down[l])
    return rms_norm(h, final_norm)[:, N_META:]
```

```python
import contextlib
import numpy as np
import concourse.bass as bass
import concourse.mybir as mybir
from concourse.bass_utils import run_bass_kernel_spmd

F32 = mybir.dt.float32
BF16 = mybir.dt.bfloat16
AF = mybir.ActivationFunctionType
ALU = mybir.AluOpType
AX = mybir.AxisListType


class Res:
    __slots__ = ("name", "w", "r", "dsem", "dram", "nobar")

    def __init__(self, name, dram=False):
        self.name = name
        self.nobar = False
        self.dram = dram
        self.w = None
        self.r = []
        self.dsem = None


class Tok:
    __slots__ = ("eng", "sem", "val")

    def __init__(self, eng, sem, val):
        self.eng, self.sem, self.val = eng, sem, val


class SemRec:
    __slots__ = ("h", "count", "id", "kind", "nobar")

    def __init__(self, h, id_):
        self.h, self.count, self.id, self.kind, self.nobar = h, 0, id_, None, False


class Sched:
    EPOCH = 30000
    POOL_DMA_CAP = 4

    def __init__(self, nc, stack):
        self.nc = nc
        self.stack = stack
        self.eng = {"pe": nc.tensor, "act": nc.scalar, "dve": nc.vector, "pool": nc.gpsimd,
                    "sp": nc.sync}
        self.nsem = 0
        self.esem = {e: self._newsem(e) for e in self.eng}
        self.known = {e: {} for e in self.eng}
        self.last = {e: None for e in self.eng}
        self.dma_sems = []
        self.free_dma = {"sw": [], "hw": []}
        self.pool_hist = []
        self.bar = {e: [] for e in self.eng}
        self.pend = {e: ([], []) for e in self.eng}
        self.nops = 0

    def _newsem(self, name):
        h = self.stack.enter_context(self.nc.semaphore(f"s{self.nsem}_{name}"))
        self.nsem += 1
        return SemRec(h, self.nsem)

    def res(self, name):
        return Res(name)

    def _wait(self, e, tok):
        if tok is None:
            return
        k = self.known[e]
        if k.get(tok.sem.id, 0) >= tok.val:
            return
        self.eng[e].wait_ge(tok.sem.h, tok.val)
        k[tok.sem.id] = tok.val

    def add(self, e, fn, r=(), w=(), sig=True, dma=None):
        r = [x for x in r if not x.dram]
        w = [x for x in w if not x.dram]
        deps = []
        for x in r:
            if x.w is not None:
                deps.append(x.w)
        for x in w:
            if x.w is not None:
                deps.append(x.w)
            deps.extend(x.r)
        deps.extend(self.bar[e])
        self.bar[e] = []
        if dma is not None and e == "pool" and len(self.pool_hist) >= self.POOL_DMA_CAP:
            deps.append(self.pool_hist[-self.POOL_DMA_CAP])
        best = {}
        for t in deps:
            if t.eng == "pe" and e == "pe" and dma is None:
                continue
            b = best.get(t.sem.id)
            if b is None or t.val > b.val:
                best[t.sem.id] = t
        for t in best.values():
            self._wait(e, t)
        ins = fn(self.eng[e])
        self.nops += 1
        tok = None
        if dma is not None:
            kind = "sw" if e == "pool" else "hw"
            if dma.dsem is None:
                if self.free_dma[kind]:
                    dma.dsem = self.free_dma[kind].pop()
                else:
                    dma.dsem = self._newsem("d_" + dma.name)
                    dma.dsem.kind = kind
                    self.dma_sems.append(dma.dsem)
            assert dma.dsem.kind == kind, (dma.name, kind)
            dma.dsem.nobar = dma.nobar
            s = dma.dsem
            s.count += 16
            ins.then_inc(s.h, 16)
            tok = Tok("dma", s, s.count)
            if e == "pool":
                self.pool_hist.append(tok)
                del self.pool_hist[:-8]
        elif sig:
            s = self.esem[e]
            if s.count >= self.EPOCH:
                s = self.esem[e] = self._newsem(e)
            s.count += 1
            ins.then_inc(s.h, 1)
            tok = Tok(e, s, s.count)
            self.last[e] = tok
        if tok is not None:
            pr, pw = self.pend[e] if dma is None else ([], [])
            if dma is None:
                self.pend[e] = ([], [])
            for x in list(r) + pr:
                x.r.append(tok)
            for x in list(w) + pw:
                x.w = tok
                x.r = []
        else:
            self.pend[e][0].extend(r)
            self.pend[e][1].extend(w)
        return tok

    def barrier(self):
        toks = [t for t in self.last.values() if t is not None]
        toks += [Tok("dma", s, s.count) for s in self.dma_sems if s.count > 0 and not s.nobar]
        for e in self.eng:
            self.bar[e] = list(toks)

    def finish(self):
        self.barrier()
        for t in self.bar["sp"]:
            self._wait("sp", t)
        self.bar["sp"] = []


D = 1024
DFF = 2816
L_TOT = 8208
T_CORE = 4104
NT = 456
EPS = 1e-6


class Tl:
    __slots__ = ("t", "res")

    def __init__(self, t, res):
        self.t, self.res = t, res


class Ctx:
    def __init__(self, nc, stack):
        self.nc = nc
        self.S = Sched(nc, stack)
        self.root = stack
        self.phase = None
        self.plist = []
        self.n = 0
        self.ps = []
        for i in range(8):
            t = stack.enter_context(nc.psum_tensor(f"psb{i}", [128, 512], F32))
            self.ps.append(Tl(t, Res(f"psb{i}")))
        self.psi = 0
        self.ones = self.tile([128, 128], BF16, "ones", root=True)
        self.S.add("dve", lambda e: e.memset(self.ones.t[:], 1.0), w=[self.ones.res])
        self.eps = self.tile([128, 1], F32, "eps", root=True)
        self.S.add("dve", lambda e: e.memset(self.eps.t[:], EPS), w=[self.eps.res])

    def tile(self, shape, dt, name, root=False):
        self.n += 1
        st = self.root if (root or self.phase is None) else self.phase
        t = st.enter_context(self.nc.sbuf_tensor(f"{name}_{self.n}", list(shape), dt))
        tl = Tl(t, Res(f"{name}_{self.n}"))
        if st is not self.root:
            self.plist[-1].append(tl.res)
        return tl

    def subres(self, n, name="part"):
        rs = [Res(f"{name}{i}") for i in range(n)]
        if self.plist:
            self.plist[-1].extend(rs)
        return rs

    def tiles(self, k, shape, dt, name):
        return [self.tile(shape, dt, f"{name}{i}") for i in range(k)]

    def psum(self):
        p = self.ps[self.psi]
        self.psi = (self.psi + 1) % 5
        return p

    def psum_acc(self):
        self.pai = 1 - getattr(self, "pai", 0)
        return self.ps[6 + self.pai]

    @contextlib.contextmanager
    def newphase(self):
        self.S.barrier()
        with contextlib.ExitStack() as st:
            old, self.phase = self.phase, st
            self.plist.append([])
            yield
            self.phase = old
        self.S.barrier()
        for r in self.plist.pop():
            if r.dsem is not None:
                r.dsem.nobar = False
                self.S.free_dma[r.dsem.kind].append(r.dsem)
                r.dsem = None

    def dres(self, name):
        return Res(name, dram=True)


def token_tiles(T, nt=NT):
    return [(t0, min(nt, T - t0)) for t0 in range(0, T, nt)]


def load_vec_fm(cx, v_dram, nch, name):
    t = cx.tile([128, nch], F32, name)
    cx.S.add("sp", lambda e: e.dma_start(out=t.t[:], in_=v_dram.rearrange("(c p) -> p c", p=128),
                                         allow_slow_non_contiguous=True), w=[t.res], dma=t.res)
    return t


def norm_to_resident(cx, h_dram, h_res, gamma, xn, T, kc=8, inv_n=1.0 / D):
    S = cx.S
    hv = h_dram.rearrange("(c p) t -> p c t", p=128)
    hts = cx.tiles(2, [128, kc, NT], F32, "nh")
    sqs = cx.tiles(2, [128, kc, NT], BF16, "nsq")
    rts = cx.tiles(2, [128, NT], F32, "nrs")
    for i, (t0, nt) in enumerate(token_tiles(T)):
        ht, sq, rt = hts[i % 2], sqs[i % 2], rts[i % 2]
        S.add("sp", lambda e: e.dma_start(out=ht.t[:, :, :nt], in_=hv[:, :, t0:t0 + nt]),
              r=[h_res], w=[ht.res], dma=ht.res)
        S.add("act", lambda e: e.activation(out=sq.t[:, :, :nt], in_=ht.t[:, :, :nt], func=AF.Square),
              r=[ht.res], w=[sq.res])
        ps = cx.psum()
        for c in range(kc):
            S.add("pe", lambda e: e.matmul(ps.t[:, :nt], lhsT=cx.ones.t[:], rhs=sq.t[:, c, :nt],
                                           start=(c == 0), stop=(c == kc - 1)),
                  r=[sq.res, cx.ones.res], w=[ps.res], sig=(c == kc - 1))
        S.add("act", lambda e: e.activation(out=rt.t[:, :nt], in_=ps.t[:, :nt], func=AF.Sqrt,
                                            bias=cx.eps.t[:], scale=inv_n), r=[ps.res, cx.eps.res], w=[rt.res])
        S.add("dve", lambda e: e.reciprocal(out=rt.t[:, :nt], in_=rt.t[:, :nt]), r=[rt.res], w=[rt.res])
        for c in range(kc):
            S.add("dve", lambda e: e.scalar_tensor_tensor(
                out=xn.t[:, c, t0:t0 + nt], in0=ht.t[:, c, :nt], scalar=gamma.t[:, c:c + 1],
                in1=rt.t[:, :nt], op0=ALU.mult, op1=ALU.mult),
                r=[ht.res, gamma.res, rt.res], w=[xn.res], sig=(c == kc - 1))


def gemm_resident(cx, xn, kc, W, groups, T, epi, epi_tile=None, nslots=3, wname="w"):
    S = cx.S
    gw = max(sum(w for _, w in g) for g in groups)
    slots = cx.tiles(nslots, [128, kc, gw], BF16, wname)
    npart = max(len(g) for g in groups)
    sparts = [cx.subres(npart, wname + "p") for _ in range(nslots)]
    Wv = W.rearrange("(c p) m -> p c m", p=128)

    def load(gi):
        sl = slots[gi % nslots]
        off = 0
        for pi, (c0, w) in enumerate(groups[gi]):
            o = off
            S.add("pool", lambda e: e.dma_start(out=sl.t[:, :, o:o + w], in_=Wv[:, :, c0:c0 + w]),
                  w=[sl.res], dma=sl.res)
            off += w

    load(0)
    for gi, g in enumerate(groups):
        if gi + 1 < len(groups):
            load(gi + 1)
        sl = slots[gi % nslots]
        for (t0, nt) in token_tiles(T):
            off = 0
            for ci, (c0, w) in enumerate(g):
                ps = cx.psum()
                for k in range(kc):
                    o = off
                    S.add("pe", lambda e: e.matmul(ps.t[:w, :nt], lhsT=sl.t[:, k, o:o + w],
                                                   rhs=xn.t[:, k, t0:t0 + nt],
                                                   start=(k == 0), stop=(k == kc - 1)),
                          r=[sl.res, xn.res], w=[ps.res], sig=(k == kc - 1))
                epi(gi, ci, ps, t0, nt)
                off += w
            if epi_tile is not None:
                epi_tile(gi, t0, nt)


def ffn_block(cx, h_in, h_in_res, h_out, h_out_res, gamma_d, w_gu, w_down, aT, aT_res, T):
    S = cx.S
    nj = DFF // 128
    outer = cx.newphase()
    outer.__enter__()
    wd = cx.tile([128, nj, D], BF16, "wd")
    wdv = w_down.rearrange("(c p) m -> p c m", p=128)
    wdp = cx.subres(nj, "wdp")
    for q in range(nj):
        wdp[q].nobar = True
        S.add("pool", lambda e: e.dma_start(out=wd.t[:, q, :], in_=wdv[:, q, :]),
              r=([wdp[q - 3]] if q >= 3 else []), w=[wdp[q]], dma=wdp[q])
    if True:
     with cx.newphase():
        gamma = load_vec_fm(cx, gamma_d, 8, "gam")
        xn = cx.tile([128, 8, T], BF16, "xn")
        norm_to_resident(cx, h_in, h_in_res, gamma, xn, T)
        sgs = cx.tiles(2, [128, NT], F32, "sg")
        aos = cx.tiles(3, [128, NT], BF16, "ao")
        state = {"n": 0, "g": None}

        def epi(gi, ci, ps, t0, nt):
            if ci == 0:
                sg = sgs[state["n"] % 2]
                S.add("act", lambda e: e.activation(out=sg.t[:, :nt], in_=ps.t[:, :nt], func=AF.Silu),
                      r=[ps.res], w=[sg.res])
                state["g"] = sg
            else:
                sg = state["g"]
                ao = aos[state["n"] % 3]
                state["n"] += 1
                S.add("dve", lambda e: e.tensor_tensor(out=ao.t[:, :nt], in0=ps.t[:, :nt], in1=sg.t[:, :nt],
                                                       op=ALU.mult), r=[ps.res, sg.res], w=[ao.res])
                S.add("sp", lambda e: e.dma_start(out=aT[gi, :, t0:t0 + nt], in_=ao.t[:, :nt]),
                      r=[ao.res], w=[aT_res], dma=ao.res)

        groups = [[(j * 128, 128), (DFF + j * 128, 128)] for j in range(nj)]
        gemm_resident(cx, xn, 8, w_gu, groups, T, epi, wname="wgu")
    if True:
     with cx.newphase():
        ats = cx.tiles(2, [128, nj, NT], BF16, "at")
        hts = cx.tiles(2, [128, 8, NT], F32, "dh")
        aTv = aT.rearrange("j p t -> p j t")
        hiv = h_in.rearrange("(c p) t -> p c t", p=128)
        hov = h_out.rearrange("(c p) t -> p c t", p=128)
        tl = token_tiles(T)

        def load(i):
            t0, nt = tl[i]
            at, ht = ats[i % 2], hts[i % 2]
            S.add("sp", lambda e: e.dma_start(out=at.t[:, :, :nt], in_=aTv[:, :, t0:t0 + nt]),
                  r=[aT_res], w=[at.res], dma=at.res)
            S.add("sp", lambda e: e.dma_start(out=ht.t[:, :, :nt], in_=hiv[:, :, t0:t0 + nt]),
                  r=[h_in_res], w=[ht.res], dma=ht.res)

        load(0)
        for i, (t0, nt) in enumerate(tl):
            if i + 1 < len(tl):
                load(i + 1)
            at, ht = ats[i % 2], hts[i % 2]
            for m in range(8):
                ps = cx.psum()
                for j in range(nj):
                    S.add("pe", lambda e: e.matmul(ps.t[:, :nt], lhsT=wd.t[:, j, m * 128:(m + 1) * 128],
                                                   rhs=at.t[:, j, :nt], start=(j == 0), stop=(j == nj - 1)),
                          r=[wdp[j], at.res], w=[ps.res], sig=(j == nj - 1))
                S.add("dve", lambda e: e.scalar_tensor_tensor(out=ht.t[:, m, :nt], in0=ps.t[:, :nt], scalar=0.5,
                                                              in1=ht.t[:, m, :nt], op0=ALU.mult, op1=ALU.add),
                      r=[ps.res, ht.res], w=[ht.res])
            S.add("sp", lambda e: e.dma_start(out=hov[:, :, t0:t0 + nt], in_=ht.t[:, :, :nt]),
                  r=[ht.res], w=[h_out_res], dma=ht.res)
    outer.__exit__(None, None, None)


NH = 8
TH = T_CORE
SC_A = 96.0 ** -0.5
SC_H = 128.0 ** -0.5
O_CQ, O_CKV, O_KPE, O_HQ, O_HF, O_HI, O_HG, O_GA, O_GB, O_KSW = 0, 384, 640, 672, 1184, 1696, 2208, 2720, 3744, 4768
DRAMR = Res("dram", dram=True)


def sub_norm(cx, src, kc, n, gam, dst, t0, nt, sq, rt):
    S = cx.S
    S.add("act", lambda e: e.activation(out=sq.t[:, :kc, :nt], in_=src.t[:, :kc, :nt], func=AF.Square),
          r=[src.res], w=[sq.res])
    ps = cx.psum()
    for c in range(kc):
        S.add("pe", lambda e: e.matmul(ps.t[:, :nt], lhsT=cx.ones.t[:], rhs=sq.t[:, c, :nt],
                                       start=(c == 0), stop=(c == kc - 1)),
              r=[sq.res, cx.ones.res], w=[ps.res], sig=(c == kc - 1))
    S.add("act", lambda e: e.activation(out=rt.t[:, :nt], in_=ps.t[:, :nt], func=AF.Sqrt,
                                        bias=cx.eps.t[:], scale=1.0 / n), r=[ps.res, cx.eps.res], w=[rt.res])
    S.add("dve", lambda e: e.reciprocal(out=rt.t[:, :nt], in_=rt.t[:, :nt]), r=[rt.res], w=[rt.res])
    for c in range(kc):
        S.add("dve", lambda e: e.scalar_tensor_tensor(
            out=dst.t[:, c, t0:t0 + nt], in0=src.t[:, c, :nt], scalar=gam.t[:, c:c + 1],
            in1=rt.t[:, :nt], op0=ALU.mult, op1=ALU.mult),
            r=[src.res, gam.res, rt.res], w=[dst.res], sig=(c == kc - 1))


def mix_pre(cx, lyr, Wl, G, th0):
    S = cx.S
    T = TH
    hT = G["hT"][:, th0:th0 + T]
    with cx.newphase():
        gamma = load_vec_fm(cx, Wl["mix_norm"], 8, "mg")
        qg = load_vec_fm(cx, Wl["q_norm"], 3, "qg")
        kg = load_vec_fm(cx, Wl["kv_norm"], 2, "kg")
        u = cx.tile([128, 8, T], BF16, "u")
        cqn = cx.tile([128, 3, T], BF16, "cqn")
        ckvn = cx.tile([128, 2, T], BF16, "ckvn")
        with cx.newphase():
            norm_to_resident(cx, hT, DRAMR, gamma, u, T)
        lbr = cx.tile([128, 4, 4], F32, "lbr")
        S.add("sp", lambda e: e.dma_start(out=lbr.t[:], in_=G["lb_raw"].rearrange("l (c p) -> p l c", p=128),
                                          allow_slow_non_contiguous=True), w=[lbr.res], dma=lbr.res)
        S.add("act", lambda e: e.activation(out=lbr.t[:], in_=lbr.t[:], func=AF.Exp), r=[lbr.res], w=[lbr.res])
        den = cx.tile([128, 4], F32, "lbden")
        lb = cx.tile([128, 4], F32, "lb")
        oml = cx.tile([128, 4], F32, "oml")
        S.add("dve", lambda e: e.tensor_tensor(out=den.t[:], in0=lbr.t[:, 0, :], in1=lbr.t[:, 1, :], op=ALU.add),
              r=[lbr.res], w=[den.res])
        for l in (2, 3):
            S.add("dve", lambda e: e.tensor_tensor(out=den.t[:], in0=den.t[:], in1=lbr.t[:, l, :], op=ALU.add),
                  r=[lbr.res, den.res], w=[den.res])
        S.add("dve", lambda e: e.reciprocal(out=den.t[:], in_=den.t[:]), r=[den.res], w=[den.res])
        S.add("dve", lambda e: e.memset(lb.t[:], 0.0), w=[lb.res])
        for l in range(1, lyr + 1):
            S.add("dve", lambda e: e.tensor_tensor(out=lb.t[:], in0=lb.t[:], in1=lbr.t[:, l, :], op=ALU.add),
                  r=[lbr.res, lb.res], w=[lb.res])
        S.add("dve", lambda e: e.tensor_tensor(out=lb.t[:], in0=lb.t[:], in1=den.t[:], op=ALU.mult),
              r=[lb.res, den.res], w=[lb.res])
        S.add("dve", lambda e: e.tensor_scalar(out=oml.t[:], in0=lb.t[:], scalar1=-1.0, scalar2=1.0,
                                               op0=ALU.mult, op1=ALU.add), r=[lb.res], w=[oml.res])

        raws = cx.tiles(2, [128, 3, NT], F32, "raw")
        sqs = cx.tiles(2, [128, 3, NT], BF16, "sq3")
        rts = cx.tiles(2, [128, NT], F32, "rt3")
        obs = cx.tiles(4, [128, NT], BF16, "ob")
        ofs = cx.tiles(2, [128, NT], F32, "of")
        fts = cx.tiles(2, [128, NT], F32, "ft")
        rps = cx.tiles(2, [128, 2, NT], F32, "rope")
        st = {"n": 0, "a": None}
        ropev = G["rope"]

        def out_bf(ps, rows, func, dst, t0, nt, scale=1.0):
            ob = obs[st["n"] % 4]
            st["n"] += 1
            S.add("act", lambda e: e.activation(out=ob.t[:rows, :nt], in_=ps.t[:rows, :nt], func=func, scale=scale),
                  r=[ps.res], w=[ob.res])
            store(ob, dst, rows, t0, nt)

        def store(ob, dst, rows, t0, nt):
            pieces = dst if isinstance(dst, list) else [(0, rows, dst)]
            for (r0, nr, d) in pieces:
                S.add("sp", lambda e: e.dma_start(out=d[:, th0 + t0:th0 + t0 + nt], in_=ob.t[r0:r0 + nr, :nt]),
                      r=[ob.res], dma=ob.res)

        def rope_out(psa, psb, rows, dst, t0, nt):
            rp = rps[st["n"] % 2]
            of = ofs[st["n"] % 2]
            ob = obs[st["n"] % 4]
            st["n"] += 1
            S.add("sp", lambda e: e.dma_start(out=rp.t[:rows, :, :nt], in_=ropev[:rows, :, th0 + t0:th0 + t0 + nt]),
                  w=[rp.res], dma=rp.res)
            S.add("dve", lambda e: e.tensor_tensor(out=of.t[:rows, :nt], in0=psa.t[:rows, :nt], in1=rp.t[:rows, 0, :nt],
                                                   op=ALU.mult), r=[psa.res, rp.res], w=[of.res])
            S.add("dve", lambda e: e.tensor_tensor(out=rp.t[:rows, 1, :nt], in0=psb.t[:rows, :nt], in1=rp.t[:rows, 1, :nt],
                                                   op=ALU.mult), r=[psb.res, rp.res], w=[rp.res])
            S.add("dve", lambda e: e.tensor_tensor(out=ob.t[:rows, :nt], in0=of.t[:rows, :nt], in1=rp.t[:rows, 1, :nt],
                                                   op=ALU.add), r=[of.res, rp.res], w=[ob.res])
            store(ob, dst, rows, t0, nt)

        groups, kinds = [], []
        groups.append([(O_CQ + 128 * i, 128) for i in range(3)]); kinds.append(("cq",))
        groups.append([(O_CKV + 128 * i, 128) for i in range(2)]); kinds.append(("ckv",))
        groups.append([(O_KPE, 32), (O_KSW, 32)]); kinds.append(("kpe",))
        for i in range(4):
            groups.append([(O_HQ + 128 * i, 128), (O_HG + 128 * i, 128)]); kinds.append(("hqg", i))
        for i in range(4):
            groups.append([(O_HF + 128 * i, 128)]); kinds.append(("hf", i))
        for i in range(8):
            groups.append([(O_GA + 128 * i, 128), (O_GB + 128 * i, 128)]); kinds.append(("gab", i))

        def epi(gi, ci, ps, t0, nt):
            kd = kinds[gi]
            if kd[0] in ("cq", "ckv"):
                raw = raws[(t0 // NT) % 2]
                S.add("act", lambda e: e.copy(out=raw.t[:, ci, :nt], in_=ps.t[:, :nt]), r=[ps.res], w=[raw.res])
            elif kd[0] == "kpe":
                if ci == 0:
                    st["a"] = ps
                else:
                    rope_out(st["a"], ps, 32, G["kpeT"], t0, nt)
            elif kd[0] == "hqg":
                i = kd[1]
                dst = G["hqT"] if ci == 0 else G["hgT"]
                out_bf(ps, 128, AF.Silu, dst[i * 128:(i + 1) * 128, :], t0, nt)
            elif kd[0] == "gab":
                i = kd[1]
                dst = G["gaT"] if ci == 0 else G["gbT"]
                out_bf(ps, 128, AF.Sigmoid, dst[i * 128:(i + 1) * 128, :], t0, nt)
            elif kd[0] == "hf":
                i = kd[1]
                ft = fts[st["n"] % 2]
                of = ofs[st["n"] % 2]
                ob = obs[st["n"] % 4]
                st["n"] += 1
                S.add("act", lambda e: e.activation(out=ft.t[:, :nt], in_=ps.t[:, :nt], func=AF.Sigmoid),
                      r=[ps.res], w=[ft.res])
                S.add("dve", lambda e: e.tensor_scalar(out=ft.t[:, :nt], in0=ft.t[:, :nt], scalar1=oml.t[:, i:i + 1],
                                                       scalar2=lb.t[:, i:i + 1], op0=ALU.mult, op1=ALU.add),
                      r=[ft.res, oml.res, lb.res], w=[ft.res])
                S.add("dve", lambda e: e.tensor_scalar(out=ft.t[:, :nt], in0=ft.t[:, :nt], scalar1=1e-20, scalar2=None,
                                                       op0=ALU.max), r=[ft.res], w=[ft.res])
                S.add("act", lambda e: e.activation(out=of.t[:, :nt], in_=ft.t[:, :nt], func=AF.Ln),
                      r=[ft.res], w=[of.res])
                S.add("sp", lambda e: e.dma_start(out=G["logfT"][i * 128:(i + 1) * 128, th0 + t0:th0 + t0 + nt],
                                                  in_=of.t[:, :nt]), r=[of.res], dma=of.res)
                S.add("dve", lambda e: e.tensor_scalar(out=ob.t[:, :nt], in0=ft.t[:, :nt], scalar1=-1.0, scalar2=1.0,
                                                       op0=ALU.mult, op1=ALU.add), r=[ft.res], w=[ob.res])
                S.add("sp", lambda e: e.dma_start(out=G["kkT"][i * 128:(i + 1) * 128, th0 + t0:th0 + t0 + nt],
                                                  in_=ob.t[:, :nt]), r=[ob.res], dma=ob.res)

        def epi_tile(gi, t0, nt):
            kd = kinds[gi]
            i = (t0 // NT) % 2
            if kd[0] == "cq":
                sub_norm(cx, raws[i], 3, 384.0, qg, cqn, t0, nt, sqs[i], rts[i])
            elif kd[0] == "ckv":
                sub_norm(cx, raws[i], 2, 256.0, kg, ckvn, t0, nt, sqs[i], rts[i])

        gemm_resident(cx, u, 8, Wl["w_in"], groups, T, epi, epi_tile, wname="win")

        whi = cx.tile([128, 8, 512], BF16, "whi")
        wv = cx.tile([128, 2, 512], BF16, "wv")
        winv = Wl["w_in"].rearrange("(c p) m -> p c m", p=128)
        whip, wvp = cx.subres(8, "whip"), cx.subres(2, "wvp")
        for c in range(8):
            S.add("pool", lambda e: e.dma_start(out=whi.t[:, c, :], in_=winv[:, c, O_HI:O_HI + 512]),
                  r=([whip[c - 3]] if c >= 3 else []), w=[whip[c]], dma=whip[c])
        wkv = Wl["w_ukv"].rearrange("(c p) m -> p c m", p=128)
        for c in range(2):
            S.add("pool", lambda e: e.dma_start(out=wv.t[:, c, :], in_=wkv[:, c, 512:1024]), w=[wvp[c]], dma=wvp[c])
        tms = cx.tiles(3, [128, 512], BF16, "tm")
        k = 0
        for b0 in range(0, T, 128):
            nb = min(128, T - b0)
            for (src, kc, wt, wtp, dst) in ((u, 8, whi, whip, G["hiTM"]), (ckvn, 2, wv, wvp, G["vTM"])):
                ps = cx.psum()
                for c in range(kc):
                    S.add("pe", lambda e: e.matmul(ps.t[:nb, :], lhsT=src.t[:, c, b0:b0 + nb], rhs=wt.t[:, c, :],
                                                   start=(c == 0), stop=(c == kc - 1)),
                          r=[src.res, wtp[c]], w=[ps.res], sig=(c == kc - 1))
                tm = tms[k % 3]
                k += 1
                S.add("act" if k % 2 else "dve",
                      (lambda e: e.copy(out=tm.t[:nb, :], in_=ps.t[:nb, :])) if k % 2 else
                      (lambda e: e.tensor_copy(out=tm.t[:nb, :], in_=ps.t[:nb, :])), r=[ps.res], w=[tm.res])
                S.add("sp", lambda e: e.dma_start(out=dst[th0 + b0:th0 + b0 + nb, :], in_=tm.t[:nb, :]),
                      r=[tm.res], dma=tm.res)

        qT, kT = G["qT"], G["kT"]
        gq = [[(128 * c, 128)] for c in range(4)] + [[(512 + 128 * c, 128), (768 + 128 * c, 128)] for c in range(2)]

        def epi_q(gi, ci, ps, t0, nt):
            if gi < 4:
                out_bf(ps, 128, AF.Copy, [(64 * j, 64, qT[2 * gi + j, 0:64, :]) for j in range(2)], t0, nt)
            elif ci == 0:
                st["a"] = ps
            else:
                c = gi - 4
                rope_out(st["a"], ps, 128, [(32 * j, 32, qT[4 * c + j, 64:96, :]) for j in range(4)], t0, nt)

        gemm_resident(cx, cqn, 3, Wl["w_uq"], gq, T, epi_q, wname="wuq")

        def epi_k(gi, ci, ps, t0, nt):
            out_bf(ps, 128, AF.Copy, [(64 * j, 64, kT[2 * gi + j, 0:64, :]) for j in range(2)], t0, nt)

        gemm_resident(cx, ckvn, 2, Wl["w_ukv"], [[(128 * c, 128)] for c in range(4)], T, epi_k, wname="wuk")


LP = 8320
NKT = LP // 128


def attention_gen(cx, G, C):
    S = cx.S
    L = L_TOT
    KT = cx.tiles(2, [96, LP], BF16, "KT")
    QT = cx.tiles(2, [96, LP], BF16, "QT")
    V = cx.tiles(2, [128, NKT, 65], BF16, "V")
    pts = cx.tiles(5, [128, 512], BF16, "pt")
    osb = cx.tiles(2, [65, 512], F32, "osb")
    rds = cx.tiles(2, [64, 512], F32, "rd")
    oos = cx.tiles(2, [64, 512], BF16, "oo")
    sel = cx.tile([65, 64], F32, "sel")
    S.add("pool", lambda e: e.memset(sel.t[:], 0.0), w=[sel.res])
    S.add("pool", lambda e: e.memset(sel.t[64:65, :], 1.0), w=[sel.res])
    for i in range(2):
        S.add("pool", lambda e: e.memset(KT[i].t[:, L:LP], 0.0), w=[KT[i].res])
        S.add("pool", lambda e: e.memset(QT[i].t[:, L:LP], 0.0), w=[QT[i].res])
        S.add("pool", lambda e: e.memset(V[i].t[:], 0.0), w=[V[i].res])
        S.add("pool", lambda e: e.memset(V[i].t[:, :, 64:65], 1.0), w=[V[i].res])
    vv = G["vTM"]

    def load(h):
        kt, qt, v = KT[h % 2], QT[h % 2], V[h % 2]
        S.add("sp", lambda e: e.dma_start(out=kt.t[0:64, 0:L], in_=G["kT"][h, 0:64, :]), w=[kt.res], dma=kt.res)
        S.add("sp", lambda e: e.dma_start(out=kt.t[64:96, 0:L], in_=G["kpeT"][:, :]), w=[kt.res], dma=kt.res)
        S.add("sp", lambda e: e.dma_start(out=qt.t[:, 0:L], in_=G["qT"][h, :, :]), w=[qt.res], dma=qt.res)
        nf = L // 128
        S.add("sp", lambda e: e.dma_start(
            out=v.t[:, 0:nf, 0:64], in_=vv[0:nf * 128, h * 64:(h + 1) * 64].rearrange("(n p) d -> p n d", p=128)),
            w=[v.res], dma=v.res)
        rem = L - nf * 128
        S.add("sp", lambda e: e.dma_start(out=v.t[0:rem, nf, 0:64], in_=vv[nf * 128:L, h * 64:(h + 1) * 64]),
              w=[v.res], dma=v.res)

    load(0)
    yield
    st = {"n": 0, "q": 0}
    DEPTH_P = 2
    for h in range(NH):
        if h + 1 < NH:
            load(h + 1)
        kt, qt, v = KT[h % 2], QT[h % 2], V[h % 2]
        for q0 in range(0, LP, 512):
            w = min(512, LP - q0)
            nk = (q0 + w) // 128
            acc = cx.psum_acc()
            ptl = {}

            def score(i):
                ps = cx.psum()
                S.add("pe", lambda e: e.matmul(ps.t[:, :w], lhsT=kt.t[:, i * 128:(i + 1) * 128], rhs=qt.t[:, q0:q0 + w],
                                               start=True, stop=True), r=[kt.res, qt.res], w=[ps.res])
                pt = pts[st["n"] % 5]
                st["n"] += 1
                S.add("act", lambda e: e.activation(out=pt.t[:, :w], in_=ps.t[:, :w], func=AF.Exp, scale=SC_A),
                      r=[ps.res], w=[pt.res])
                d = i - q0 // 128
                if d >= 0:
                    S.add("pool", lambda e: e.tensor_tensor(out=pt.t[:, :w], in0=pt.t[:, :w], in1=C["amask"].t[:, d, :w],
                                                            op=ALU.mult), r=[pt.res, C["amask"].res], w=[pt.res])
                ptl[i] = pt

            for i in range(min(DEPTH_P, nk)):
                score(i)
            for i in range(nk):
                if i + DEPTH_P < nk:
                    score(i + DEPTH_P)
                pt = ptl.pop(i)
                S.add("pe", lambda e: e.matmul(acc.t[0:65, :w], lhsT=v.t[:, i, :], rhs=pt.t[:, :w],
                                               start=(i == 0), stop=(i == nk - 1)),
                      r=[v.res, pt.res], w=[acc.res], sig=(i == nk - 1))
                yield
            k = st["q"] % 2
            st["q"] += 1
            ob, rd, oo = osb[k], rds[k], oos[k]
            S.add("dve", lambda e: e.tensor_copy(out=ob.t[:, :w], in_=acc.t[0:65, :w]), r=[acc.res], w=[ob.res])
            ps = cx.psum()
            S.add("pe", lambda e: e.matmul(ps.t[0:64, :w], lhsT=sel.t[:, :], rhs=ob.t[:, :w], start=True, stop=True),
                  r=[sel.res, ob.res], w=[ps.res])
            S.add("dve", lambda e: e.reciprocal(out=rd.t[:, :w], in_=ps.t[0:64, :w]), r=[ps.res], w=[rd.res])
            S.add("dve", lambda e: e.tensor_tensor(out=oo.t[:, :w], in0=ob.t[0:64, :w], in1=rd.t[:, :w], op=ALU.mult),
                  r=[ob.res, rd.res], w=[oo.res])
            wr = min(w, L - q0)
            S.add("sp", lambda e: e.dma_start(out=G["oaT"][h * 64:(h + 1) * 64, q0:q0 + wr], in_=oo.t[:, :wr]),
                  r=[oo.res], dma=oo.res)
            yield


def hgrn_gen(cx, G, C, Wl):
    S = cx.S
    L = L_TOT
    CW = 64
    nw = cx.tile([128, 1], F32, "hgnw")
    S.add("sp", lambda e: e.dma_start(out=nw.t[:], in_=Wl["hg_norm"].rearrange("(p o) -> p o", o=1)),
          w=[nw.res], dma=nw.res)
    Sf = cx.tile([128, 128], F32, "Sf")
    Sb = cx.tile([128, 128], BF16, "Sb")
    qs = cx.tiles(2, [128, 512], BF16, "hq")
    lfs = cx.tiles(2, [128, 512], F32, "hlf")
    kks = cx.tiles(2, [128, 512], BF16, "hkk")
    gs = cx.tiles(2, [128, 512], BF16, "hgg")
    vs = cx.tiles(2, [64, 8, 128], BF16, "hv")
    bt = cx.tile([128, 512], F32, "hb")
    d1 = cx.tile([128, 512], F32, "hd1")
    e1 = cx.tile([128, 512], F32, "he1")
    e2 = cx.tile([128, 512], F32, "he2")
    qtl = cx.tile([128, 512], BF16, "hqt")
    ktl = cx.tile([128, 512], BF16, "hkt")
    qil = cx.tile([128, 512], BF16, "hqi")
    ksl = cx.tile([128, 512], BF16, "hks")
    dc = cx.tile([128, 8], F32, "hdc")
    kst = cx.tiles(2, [64, 128], BF16, "hkst")
    am = cx.tiles(2, [64, 64], BF16, "ham")
    sq = cx.tile([128, 512], BF16, "hsq")
    rt = cx.tile([128, 512], F32, "hrt")
    on = cx.tile([128, 512], F32, "hon")
    ou = cx.tiles(2, [128, 512], BF16, "hou")
    segs = [(s0, min(512, L - s0)) for s0 in range(0, L, 512)]
    acc = cx.ps[5]

    def load(hd, si):
        s0, w = segs[si]
        k = si % 2
        r = slice(hd * 128, (hd + 1) * 128)
        S.add("sp", lambda e: e.dma_start(out=qs[k].t[:, :w], in_=G["hqT"][r, s0:s0 + w]), w=[qs[k].res], dma=qs[k].res)
        S.add("sp", lambda e: e.dma_start(out=lfs[k].t[:, :w], in_=G["logfT"][r, s0:s0 + w]), w=[lfs[k].res], dma=lfs[k].res)
        S.add("sp", lambda e: e.dma_start(out=kks[k].t[:, :w], in_=G["kkT"][r, s0:s0 + w]), w=[kks[k].res], dma=kks[k].res)
        S.add("sp", lambda e: e.dma_start(out=gs[k].t[:, :w], in_=G["hgT"][r, s0:s0 + w]), w=[gs[k].res], dma=gs[k].res)
        cw = min(CW, w)
        nc_ = w // cw
        S.add("sp", lambda e: e.dma_start(
            out=vs[k].t[:cw, :nc_, :], in_=G["hiTM"][s0:s0 + w, r].rearrange("(n s) v -> s n v", s=cw)),
            w=[vs[k].res], dma=vs[k].res)

    yield
    for hd in range(4):
        S.add("dve", lambda e: e.memset(Sf.t[:], 0.0), w=[Sf.res])
        S.add("dve", lambda e: e.memset(Sb.t[:], 0.0), w=[Sb.res])
        load(hd, 0)
        for si, (s0, w) in enumerate(segs):
            if si + 1 < len(segs):
                load(hd, si + 1)
            k = si % 2
            q, lf, kk, g, v = qs[k], lfs[k], kks[k], gs[k], vs[k]
            cw = min(CW, w)
            nch = w // cw
            mid = cw // 2 - 1
            b3 = bt.t[:, :w].rearrange("p (n s) -> p n s", s=cw)
            d3 = d1.t[:, :w].rearrange("p (n s) -> p n s", s=cw)
            S.add("dve", lambda e: e.tensor_tensor_scan(out=bt.t[:, :w], data0=C["cm"].t[:, :w], data1=lf.t[:, :w],
                                                        initial=0.0, op0=ALU.mult, op1=ALU.add),
                  r=[C["cm"].res, lf.res], w=[bt.res])
            S.add("dve", lambda e: e.tensor_tensor(out=d3, in0=b3, in1=b3[:, :, mid:mid + 1].to_broadcast([128, nch, cw]),
                                                   op=ALU.subtract), r=[bt.res], w=[d1.res])
            S.add("dve", lambda e: e.tensor_scalar(out=d1.t[:, :w], in0=d1.t[:, :w], scalar1=-43.0, scalar2=43.0,
                                                   op0=ALU.max, op1=ALU.min), r=[d1.res], w=[d1.res])
            S.add("act", lambda e: e.activation(out=e1.t[:, :w], in_=d1.t[:, :w], func=AF.Exp), r=[d1.res], w=[e1.res])
            S.add("act", lambda e: e.activation(out=e2.t[:, :w], in_=d1.t[:, :w], func=AF.Exp, scale=-1.0),
                  r=[d1.res], w=[e2.res])
            S.add("dve", lambda e: e.scalar_tensor_tensor(out=qtl.t[:, :w], in0=e1.t[:, :w], scalar=SC_H, in1=q.t[:, :w],
                                                          op0=ALU.mult, op1=ALU.mult), r=[e1.res, q.res], w=[qtl.res])
            S.add("pool", lambda e: e.tensor_tensor(out=ktl.t[:, :w], in0=e2.t[:, :w], in1=kk.t[:, :w], op=ALU.mult),
                  r=[e2.res, kk.res], w=[ktl.res])
            yield
            S.add("act", lambda e: e.activation(out=e1.t[:, :w], in_=bt.t[:, :w], func=AF.Exp), r=[bt.res], w=[e1.res])
            S.add("dve", lambda e: e.scalar_tensor_tensor(out=qil.t[:, :w], in0=e1.t[:, :w], scalar=SC_H, in1=q.t[:, :w],
                                                          op0=ALU.mult, op1=ALU.mult), r=[e1.res, q.res], w=[qil.res])
            S.add("dve", lambda e: e.tensor_tensor(out=d3, in0=b3, in1=b3[:, :, cw - 1:cw].to_broadcast([128, nch, cw]),
                                                   op=ALU.subtract), r=[bt.res], w=[d1.res])
            S.add("act", lambda e: e.activation(out=e2.t[:, :w], in_=d1.t[:, :w], func=AF.Exp, scale=-1.0),
                  r=[d1.res], w=[e2.res])
            S.add("pool", lambda e: e.tensor_tensor(out=ksl.t[:, :w], in0=e2.t[:, :w], in1=kk.t[:, :w], op=ALU.mult),
                  r=[e2.res, kk.res], w=[ksl.res])
            S.add("act", lambda e: e.activation(out=dc.t[:, :nch], in_=b3[:, :, cw - 1], func=AF.Exp),
                  r=[bt.res], w=[dc.res])
            yield
            for n in range(nch):
                c0 = n * cw
                ps = cx.psum()
                S.add("pe", lambda e: e.matmul(ps.t[:cw, 0:128], lhsT=ksl.t[:, c0:c0 + cw], rhs=C["ident"].t[:, :],
                                               start=True, stop=True), r=[ksl.res, C["ident"].res], w=[ps.res])
                ks_ = kst[n % 2]
                S.add("dve", lambda e: e.tensor_copy(out=ks_.t[:cw, :], in_=ps.t[:cw, 0:128]), r=[ps.res], w=[ks_.res])
                ps2 = cx.psum()
                S.add("pe", lambda e: e.matmul(ps2.t[:cw, :cw], lhsT=ktl.t[:, c0:c0 + cw], rhs=qtl.t[:, c0:c0 + cw],
                                               start=True, stop=True), r=[ktl.res, qtl.res], w=[ps2.res])
                am_ = am[n % 2]
                S.add("dve", lambda e: e.tensor_tensor(out=am_.t[:cw, :cw], in0=ps2.t[:cw, :cw], in1=C["hmask"].t[:cw, :cw],
                                                       op=ALU.mult), r=[ps2.res, C["hmask"].res], w=[am_.res])
                yield
                S.add("pe", lambda e: e.matmul(acc.t[:, c0:c0 + cw], lhsT=v.t[:cw, n, :], rhs=am_.t[:cw, :cw],
                                               start=True, stop=False), r=[v.res, am_.res], w=[acc.res], sig=False)
                S.add("pe", lambda e: e.matmul(acc.t[:, c0:c0 + cw], lhsT=Sb.t[:, :], rhs=qil.t[:, c0:c0 + cw],
                                               start=False, stop=True), r=[Sb.res, qil.res], w=[acc.res])
                ps3 = cx.psum()
                S.add("pe", lambda e: e.matmul(ps3.t[:, 0:128], lhsT=ks_.t[:cw, :], rhs=v.t[:cw, n, :],
                                               start=True, stop=True), r=[ks_.res, v.res], w=[ps3.res])
                S.add("dve", lambda e: e.scalar_tensor_tensor(out=Sf.t[:], in0=Sf.t[:], scalar=dc.t[:, n:n + 1],
                                                              in1=ps3.t[:, 0:128], op0=ALU.mult, op1=ALU.add),
                      r=[Sf.res, dc.res, ps3.res], w=[Sf.res])
                S.add("pool", lambda e: e.tensor_copy(out=Sb.t[:], in_=Sf.t[:]), r=[Sf.res], w=[Sb.res])
                yield
            S.add("act", lambda e: e.activation(out=sq.t[:, :w], in_=acc.t[:, :w], func=AF.Square), r=[acc.res], w=[sq.res])
            ps = cx.psum()
            S.add("pe", lambda e: e.matmul(ps.t[:, :w], lhsT=cx.ones.t[:], rhs=sq.t[:, :w], start=True, stop=True),
                  r=[sq.res, cx.ones.res], w=[ps.res])
            S.add("dve", lambda e: e.tensor_scalar(out=on.t[:, :w], in0=acc.t[:, :w], scalar1=nw.t[:, 0:1], scalar2=None,
                                                   op0=ALU.mult), r=[acc.res, nw.res, sq.res], w=[on.res])
            S.add("act", lambda e: e.activation(out=rt.t[:, :w], in_=ps.t[:, :w], func=AF.Sqrt, bias=cx.eps.t[:],
                                                scale=1.0 / 128), r=[ps.res, cx.eps.res], w=[rt.res])
            yield
            S.add("dve", lambda e: e.reciprocal(out=rt.t[:, :w], in_=rt.t[:, :w]), r=[rt.res], w=[rt.res])
            S.add("pool", lambda e: e.tensor_tensor(out=on.t[:, :w], in0=on.t[:, :w], in1=g.t[:, :w], op=ALU.mult),
                  r=[on.res, g.res], w=[on.res])
            o_ = ou[si % 2]
            S.add("pool", lambda e: e.tensor_tensor(out=o_.t[:, :w], in0=on.t[:, :w], in1=rt.t[:, :w], op=ALU.mult),
                  r=[on.res, rt.res], w=[o_.res])
            S.add("sp", lambda e: e.dma_start(out=G["orT"][hd * 128:(hd + 1) * 128, s0:s0 + w], in_=o_.t[:, :w]),
                  r=[o_.res], dma=o_.res)
            yield


def mixers(cx, G, C, Wl, ratio=6):
    import os
    ratio = int(os.environ.get("MIX_RATIO", ratio))
    with cx.newphase():
        ga = attention_gen(cx, G, C)
        gh = hgrn_gen(cx, G, C, Wl)
        next(ga)
        next(gh)
        k = 0
        alive = True
        for _ in ga:
            k += 1
            if alive and k % ratio == 0:
                try:
                    next(gh)
                except StopIteration:
                    alive = False
        if alive:
            for _ in gh:
                pass


def mix_post(cx, Wl, G, th0):
    S = cx.S
    T = TH
    with cx.newphase():
        wa = cx.tile([128, 4, D], BF16, "wa")
        wr = cx.tile([128, 4, D], BF16, "wr")
        wo = cx.tile([128, 8, D], BF16, "wo")
        wparts = {}
        for (wt, src, kc) in ((wa, Wl["w_proj_attn"], 4), (wr, Wl["w_proj_rec"], 4), (wo, Wl["w_out"], 8)):
            sv = src.rearrange("(c p) m -> p c m", p=128)
            wparts[id(wt)] = cx.subres(kc, "wpp")
            for c in range(kc):
                pr = wparts[id(wt)][c]
                S.add("pool", lambda e: e.dma_start(out=wt.t[:, c, :], in_=sv[:, c, :]),
                      r=([wparts[id(wt)][c - 3]] if c >= 3 else []), w=[pr], dma=pr)
        oas = cx.tiles(2, [128, 4, NT], BF16, "poa")
        ors = cx.tiles(2, [128, 4, NT], BF16, "por")
        gas = cx.tiles(2, [128, 8, NT], BF16, "pga")
        gbs = cx.tiles(2, [128, 8, NT], BF16, "pgb")
        hts = cx.tiles(2, [128, 8, NT], F32, "ph")
        mgs = cx.tiles(2, [128, 8, NT], BF16, "pmg")
        t1s = cx.tiles(2, [128, NT], F32, "pt1")
        t2s = cx.tiles(2, [128, NT], F32, "pt2")
        hv = G["hT"][:, th0:th0 + T].rearrange("(c p) t -> p c t", p=128)
        tl = token_tiles(T)

        def fm(name, t0, nt):
            return G[name][:, th0 + t0:th0 + t0 + nt].rearrange("(c p) t -> p c t", p=128)

        def load(i):
            t0, nt = tl[i]
            k = i % 2
            for (tt, name) in ((oas[k], "oaT"), (ors[k], "orT"), (gas[k], "gaT"), (gbs[k], "gbT")):
                S.add("sp", lambda e: e.dma_start(out=tt.t[:, :, :nt], in_=fm(name, t0, nt)), w=[tt.res], dma=tt.res)
            S.add("sp", lambda e: e.dma_start(out=hts[k].t[:, :, :nt], in_=hv[:, :, t0:t0 + nt]), w=[hts[k].res], dma=hts[k].res)

        load(0)
        n = 0
        for i, (t0, nt) in enumerate(tl):
            if i + 1 < len(tl):
                load(i + 1)
            k = i % 2
            oa, orr, ga, gb, ht, mg = oas[k], ors[k], gas[k], gbs[k], hts[k], mgs[k]
            for c in range(8):
                psa, psb = cx.psum(), cx.psum()
                for (ps, wt, xx) in ((psa, wa, oa), (psb, wr, orr)):
                    for kk in range(4):
                        S.add("pe", lambda e: e.matmul(ps.t[:, :nt], lhsT=wt.t[:, kk, c * 128:(c + 1) * 128], rhs=xx.t[:, kk, :nt],
                                                       start=(kk == 0), stop=(kk == 3)), r=[wparts[id(wt)][kk], xx.res], w=[ps.res], sig=(kk == 3))
                t1, t2 = t1s[n % 2], t2s[n % 2]
                n += 1
                S.add("dve", lambda e: e.tensor_tensor(out=t1.t[:, :nt], in0=psa.t[:, :nt], in1=ga.t[:, c, :nt], op=ALU.mult),
                      r=[psa.res, ga.res], w=[t1.res])
                S.add("dve", lambda e: e.tensor_tensor(out=t2.t[:, :nt], in0=psb.t[:, :nt], in1=gb.t[:, c, :nt], op=ALU.mult),
                      r=[psb.res, gb.res], w=[t2.res])
                S.add("pool", lambda e: e.tensor_tensor(out=mg.t[:, c, :nt], in0=t1.t[:, :nt], in1=t2.t[:, :nt], op=ALU.add),
                      r=[t1.res, t2.res], w=[mg.res])
            for m in range(8):
                ps = cx.psum()
                for kk in range(8):
                    S.add("pe", lambda e: e.matmul(ps.t[:, :nt], lhsT=wo.t[:, kk, m * 128:(m + 1) * 128], rhs=mg.t[:, kk, :nt],
                                                   start=(kk == 0), stop=(kk == 7)), r=[wparts[id(wo)][kk], mg.res], w=[ps.res], sig=(kk == 7))
                S.add("dve", lambda e: e.tensor_tensor(out=ht.t[:, m, :nt], in0=ps.t[:, :nt], in1=ht.t[:, m, :nt], op=ALU.add),
                      r=[ps.res, ht.res], w=[ht.res])
            S.add("sp", lambda e: e.dma_start(out=hv[:, :, t0:t0 + nt], in_=ht.t[:, :, :nt]), r=[ht.res], dma=ht.res)


def final_norm(cx, G, gamma_d, yT, th0):
    S = cx.S
    T = TH
    with cx.newphase():
        gamma = load_vec_fm(cx, gamma_d, 8, "fg")
        hv = G["hT"][:, th0:th0 + T].rearrange("(c p) t -> p c t", p=128)
        yv = yT.rearrange("(c p) t -> p c t", p=128)
        hts = cx.tiles(2, [128, 8, NT], F32, "fh")
        sqs = cx.tiles(2, [128, 8, NT], BF16, "fsq")
        rts = cx.tiles(2, [128, NT], F32, "frs")
        for i, (t0, nt) in enumerate(token_tiles(T)):
            ht, sq, rt = hts[i % 2], sqs[i % 2], rts[i % 2]
            S.add("sp", lambda e: e.dma_start(out=ht.t[:, :, :nt], in_=hv[:, :, t0:t0 + nt]), w=[ht.res], dma=ht.res)
            S.add("act", lambda e: e.activation(out=sq.t[:, :, :nt], in_=ht.t[:, :, :nt], func=AF.Square), r=[ht.res], w=[sq.res])
            ps = cx.psum()
            for c in range(8):
                S.add("pe", lambda e: e.matmul(ps.t[:, :nt], lhsT=cx.ones.t[:], rhs=sq.t[:, c, :nt], start=(c == 0), stop=(c == 7)),
                      r=[sq.res, cx.ones.res], w=[ps.res], sig=(c == 7))
            S.add("act", lambda e: e.activation(out=rt.t[:, :nt], in_=ps.t[:, :nt], func=AF.Sqrt, bias=cx.eps.t[:], scale=1.0 / D),
                  r=[ps.res, cx.eps.res], w=[rt.res])
            S.add("dve", lambda e: e.reciprocal(out=rt.t[:, :nt], in_=rt.t[:, :nt]), r=[rt.res], w=[rt.res])
            for c in range(8):
                S.add("dve", lambda e: e.scalar_tensor_tensor(out=ht.t[:, c, :nt], in0=ht.t[:, c, :nt], scalar=gamma.t[:, c:c + 1],
                                                              in1=rt.t[:, :nt], op0=ALU.mult, op1=ALU.mult),
                      r=[ht.res, gamma.res, rt.res], w=[ht.res], sig=(c == 7))
            g0 = th0 + t0 - 16
            skip = max(0, -g0)
            if nt - skip > 0:
                S.add("sp", lambda e: e.dma_start(out=yv[:, :, g0 + skip:g0 + nt], in_=ht.t[:, :, skip:nt]), r=[ht.res], dma=ht.res)


WNAMES = {"ffn1_norm": [D], "ffn1_w_gu": [D, 2 * DFF], "ffn1_w_down": [DFF, D], "mix_norm": [D],
          "w_in": [D, 4800], "q_norm": [384], "kv_norm": [256], "w_uq": [384, 1024], "w_ukv": [256, 1024],
          "hg_norm": [128], "w_proj_attn": [512, D], "w_proj_rec": [512, D], "w_out": [D, D],
          "ffn2_norm": [D], "ffn2_w_gu": [D, 2 * DFF], "ffn2_w_down": [DFF, D]}


def build_program(nl=4):
    nc = bass.Bass("TRN2", target_bir_lowering=False)
    L = L_TOT

    def din(name, shape, dt=F32):
        return nc.dram_tensor(name, list(shape), dt, kind="ExternalInput").ap()

    def dsc(name, shape, dt):
        return nc.dram_tensor(name, list(shape), dt, kind="Internal").ap()

    xT = din("xT", [D, L])
    Wd = {k: din(k, [nl] + v) for k, v in WNAMES.items()}
    lb_raw = din("hg_lb_raw", [4, 512])
    fin = din("final_norm", [D])
    rope = din("rope", [128, 2, L])
    amask_d = din("amask", [128, 4, 512])
    hmask_d = din("hmask", [64, 64])
    cm_d = din("cm", [128, 512])
    ident_d = din("ident", [128, 128])
    yT = nc.dram_tensor("yT", [D, L - 16], F32, kind="ExternalOutput").ap()
    G = {"hT": dsc("hT", [D, L], F32), "aT": dsc("aT", [DFF // 128, 128, TH], BF16),
         "qT": dsc("qT", [NH, 96, L], BF16), "kT": dsc("kT", [NH, 64, L], BF16), "kpeT": dsc("kpeT", [32, L], BF16),
         "vTM": dsc("vTM", [L, 512], BF16), "hqT": dsc("hqT", [512, L], BF16), "hgT": dsc("hgT", [512, L], BF16),
         "logfT": dsc("logfT", [512, L], F32), "kkT": dsc("kkT", [512, L], BF16), "hiTM": dsc("hiTM", [L, 512], BF16),
         "gaT": dsc("gaT", [D, L], BF16), "gbT": dsc("gbT", [D, L], BF16),
         "oaT": dsc("oaT", [512, L], BF16), "orT": dsc("orT", [512, L], BF16),
         "lb_raw": lb_raw, "rope": rope}
    with contextlib.ExitStack() as st:
        cx = Ctx(nc, st)
        S = cx.S
        C = {}
        for (nm, src, shape) in (("amask", amask_d, [128, 4, 512]), ("hmask", hmask_d, [64, 64]),
                                 ("ident", ident_d, [128, 128])):
            C[nm] = cx.tile(shape, BF16, nm, root=True)
            S.add("pool", lambda e: e.dma_start(out=C[nm].t[:], in_=src), w=[C[nm].res], dma=C[nm].res)
        C["cm"] = cx.tile([128, 512], F32, "cm", root=True)
        S.add("sp", lambda e: e.dma_start(out=C["cm"].t[:], in_=cm_d), w=[C["cm"].res], dma=C["cm"].res)
        dummy = Res("cp")
        for c in range(8):
            S.add("sp", lambda e: e.dma_start(out=G["hT"][c * 128:(c + 1) * 128, :], in_=xT[c * 128:(c + 1) * 128, :]), dma=dummy)
        for l in range(nl):
            Wl = {k: v[l] for k, v in Wd.items()}
            for th0 in (0, TH):
                hs = G["hT"][:, th0:th0 + TH]
                ffn_block(cx, hs, DRAMR, hs, DRAMR, Wl["ffn1_norm"], Wl["ffn1_w_gu"], Wl["ffn1_w_down"], G["aT"], DRAMR, TH)
                mix_pre(cx, l, Wl, G, th0)
            mixers(cx, G, C, Wl)
            for th0 in (0, TH):
                hs = G["hT"][:, th0:th0 + TH]
                mix_post(cx, Wl, G, th0)
                ffn_block(cx, hs, DRAMR, hs, DRAMR, Wl["ffn2_norm"], Wl["ffn2_w_gu"], Wl["ffn2_w_down"], G["aT"], DRAMR, TH)
        for th0 in (0, TH):
            final_norm(cx, G, fin, yT, th0)
        S.finish()
        build_program.stats = (S.nops, S.nsem)
    return nc


def host_consts():
    L = L_TOT
    half = 16
    inv = 10000.0 ** (-np.arange(half, dtype=np.float32) / half)
    ang = np.arange(L, dtype=np.float32)[:, None] * inv[None, :]
    cos, sin = np.cos(ang).T, np.sin(ang).T
    rope = np.zeros((128, 2, L), np.float32)
    for p in range(128):
        r = p % 32
        rope[p, 0] = cos[r % 16]
        rope[p, 1] = -sin[r % 16] if r < 16 else sin[r % 16]
    kk = np.arange(128)[:, None, None]
    dd = np.arange(4)[None, :, None]
    qq = np.arange(512)[None, None, :]
    amask = (128 * dd + kk <= qq).astype(np.float32)
    hmask = (np.arange(64)[:, None] <= np.arange(64)[None, :]).astype(np.float32)
    cm = np.ones((128, 512), np.float32)
    cm[:, ::64] = 0.0
    return {"rope": rope, "amask": np.ascontiguousarray(amask), "hmask": hmask, "cm": cm,
            "ident": np.eye(128, dtype=np.float32)}


def host_weights(inp, nl):
    W = {}
    for k in WNAMES:
        if k in ("w_in", "w_uq", "w_ukv"):
            continue
        W[k] = np.ascontiguousarray(inp[k][:nl])
    w_in = inp["w_in"][:nl]
    ksw = np.concatenate([w_in[:, :, O_KPE + 16:O_KPE + 32], w_in[:, :, O_KPE:O_KPE + 16]], axis=-1)
    W["w_in"] = np.ascontiguousarray(np.concatenate([w_in, ksw], axis=-1))
    wq = inp["w_uq"][:nl].reshape(nl, 384, NH, 96)
    nope = wq[..., :64].reshape(nl, 384, 512)
    rp = wq[..., 64:]
    rsw = np.concatenate([rp[..., 16:], rp[..., :16]], axis=-1)
    W["w_uq"] = np.ascontiguousarray(np.concatenate([nope, rp.reshape(nl, 384, 256), rsw.reshape(nl, 384, 256)], axis=-1))
    wk = inp["w_ukv"][:nl].reshape(nl, 256, NH, 128)
    W["w_ukv"] = np.ascontiguousarray(np.concatenate([wk[..., :64].reshape(nl, 256, 512), wk[..., 64:].reshape(nl, 256, 512)], axis=-1))
    return W


_PROG = {}


def run_model(inp, nl=4, batches=(0, 1, 2, 3), trace=False):
    if nl not in _PROG:
        _PROG[nl] = build_program(nl)
    nc = _PROG[nl]
    W = host_weights(inp, nl)
    cst = host_consts()
    meta = np.asarray(inp["meta_tokens"], np.float32)
    maps = []
    for b in batches:
        h0 = np.concatenate([meta, np.asarray(inp["x"][b], np.float32)], axis=0)
        m = {"xT": np.ascontiguousarray(h0.T), "hg_lb_raw": np.asarray(inp["hg_lb_raw"], np.float32),
             "final_norm": np.asarray(inp["final_norm"], np.float32)}
        m.update(W)
        m.update(cst)
        maps.append(m)
    res = run_bass_kernel_spmd(nc, maps, core_ids=list(range(len(batches))), trace=trace)
    out = np.stack([np.ascontiguousarray(r["yT"].T) for r in res.results], axis=0)
    return out.astype(np.float32), res


def kernel(**inputs):
    out, _ = run_model(inputs, nl=4)
    return out
```

```python
import contextlib
import numpy as np
import concourse.bass as bass
import concourse.mybir as mybir
from concourse.bass_utils import run_bass_kernel_spmd

F32 = mybir.dt.float32
BF16 = mybir.dt.bfloat16
AF = mybir.ActivationFunctionType
ALU = mybir.AluOpType
AX = mybir.AxisListType


class Res:
    __slots__ = ("name", "w", "r", "dsem", "dram")

    def __init__(self, name, dram=False):
        self.name = name
        self.dram = dram
        self.w = None
        self.r = []
        self.dsem = None


class Tok:
    __slots__ = ("eng", "sem", "val")

    def __init__(self, eng, sem, val):
        self.eng, self.sem, self.val = eng, sem, val


class SemRec:
    __slots__ = ("h", "count", "id", "kind")

    def __init__(self, h, id_):
        self.h, self.count, self.id, self.kind = h, 0, id_, None


class Sched:
    EPOCH = 30000
    POOL_DMA_CAP = 4

    def __init__(self, nc, stack):
        self.nc = nc
        self.stack = stack
        self.eng = {"pe": nc.tensor, "act": nc.scalar, "dve": nc.vector, "pool": nc.gpsimd,
                    "sp": nc.sync}
        self.nsem = 0
        self.esem = {e: self._newsem(e) for e in self.eng}
        self.known = {e: {} for e in self.eng}
        self.last = {e: None for e in self.eng}
        self.dma_sems = []
        self.free_dma = {"sw": [], "hw": []}
        self.pool_hist = []
        self.bar = {e: [] for e in self.eng}
        self.pend = {e: ([], []) for e in self.eng}
        self.nops = 0

    def _newsem(self, name):
        h = self.stack.enter_context(self.nc.semaphore(f"s{self.nsem}_{name}"))
        self.nsem += 1
        return SemRec(h, self.nsem)

    def res(self, name):
        return Res(name)

    def _wait(self, e, tok):
        if tok is None:
            return
        k = self.known[e]
        if k.get(tok.sem.id, 0) >= tok.val:
            return
        self.eng[e].wait_ge(tok.sem.h, tok.val)
        k[tok.sem.id] = tok.val

    def add(self, e, fn, r=(), w=(), sig=True, dma=None):
        r = [x for x in r if not x.dram]
        w = [x for x in w if not x.dram]
        deps = []
        for x in r:
            if x.w is not None:
                deps.append(x.w)
        for x in w:
            if x.w is not None:
                deps.append(x.w)
            deps.extend(x.r)
        deps.extend(self.bar[e])
        self.bar[e] = []
        if dma is not None and e == "pool" and len(self.pool_hist) >= self.POOL_DMA_CAP:
            deps.append(self.pool_hist[-self.POOL_DMA_CAP])
        best = {}
        for t in deps:
            if t.eng == "pe" and e == "pe" and dma is None:
                continue
            b = best.get(t.sem.id)
            if b is None or t.val > b.val:
                best[t.sem.id] = t
        for t in best.values():
            self._wait(e, t)
        ins = fn(self.eng[e])
        self.nops += 1
        tok = None
        if dma is not None:
            kind = "sw" if e == "pool" else "hw"
            if dma.dsem is None:
                if self.free_dma[kind]:
                    dma.dsem = self.free_dma[kind].pop()
                else:
                    dma.dsem = self._newsem("d_" + dma.name)
                    dma.dsem.kind = kind
                    self.dma_sems.append(dma.dsem)
            assert dma.dsem.kind == kind, (dma.name, kind)
            s = dma.dsem
            s.count += 16
            ins.then_inc(s.h, 16)
            tok = Tok("dma", s, s.count)
            if e == "pool":
                self.pool_hist.append(tok)
                del self.pool_hist[:-8]
        elif sig:
            s = self.esem[e]
            if s.count >= self.EPOCH:
                s = self.esem[e] = self._newsem(e)
            s.count += 1
            ins.then_inc(s.h, 1)
            tok = Tok(e, s, s.count)
            self.last[e] = tok
        if tok is not None:
            pr, pw = self.pend[e] if dma is None else ([], [])
            if dma is None:
                self.pend[e] = ([], [])
            for x in list(r) + pr:
                x.r.append(tok)
            for x in list(w) + pw:
                x.w = tok
                x.r = []
        else:
            self.pend[e][0].extend(r)
            self.pend[e][1].extend(w)
        return tok

    def barrier(self):
        toks = [t for t in self.last.values() if t is not None]
        toks += [Tok("dma", s, s.count) for s in self.dma_sems if s.count > 0]
        for e in self.eng:
            self.bar[e] = list(toks)

    def finish(self):
        self.barrier()
        for t in self.bar["sp"]:
            self._wait("sp", t)
        self.bar["sp"] = []


D = 1024
DFF = 2816
L_TOT = 8208
T_CORE = 4104
NT = 456
EPS = 1e-6


class Tl:
    __slots__ = ("t", "res")

    def __init__(self, t, res):
        self.t, self.res = t, res


class Ctx:
    def __init__(self, nc, stack):
        self.nc = nc
        self.S = Sched(nc, stack)
        self.root = stack
        self.phase = None
        self.plist = []
        self.n = 0
        self.ps = []
        for i in range(8):
            t = stack.enter_context(nc.psum_tensor(f"psb{i}", [128, 512], F32))
            self.ps.append(Tl(t, Res(f"psb{i}")))
        self.psi = 0
        self.ones = self.tile([128, 128], BF16, "ones", root=True)
        self.S.add("dve", lambda e: e.memset(self.ones.t[:], 1.0), w=[self.ones.res])
        self.eps = self.tile([128, 1], F32, "eps", root=True)
        self.S.add("dve", lambda e: e.memset(self.eps.t[:], EPS), w=[self.eps.res])

    def tile(self, shape, dt, name, root=False):
        self.n += 1
        st = self.root if (root or self.phase is None) else self.phase
        t = st.enter_context(self.nc.sbuf_tensor(f"{name}_{self.n}", list(shape), dt))
        tl = Tl(t, Res(f"{name}_{self.n}"))
        if st is not self.root:
            self.plist[-1].append(tl.res)
        return tl

    def subres(self, n, name="part"):
        rs = [Res(f"{name}{i}") for i in range(n)]
        if self.plist:
            self.plist[-1].extend(rs)
        return rs

    def tiles(self, k, shape, dt, name):
        return [self.tile(shape, dt, f"{name}{i}") for i in range(k)]

    def psum(self):
        p = self.ps[self.psi]
        self.psi = (self.psi + 1) % 5
        return p

    def psum_acc(self):
        self.pai = 1 - getattr(self, "pai", 0)
        return self.ps[6 + self.pai]

    @contextlib.contextmanager
    def newphase(self):
        self.S.barrier()
        with contextlib.ExitStack() as st:
            old, self.phase = self.phase, st
            self.plist.append([])
            yield
            self.phase = old
        self.S.barrier()
        for r in self.plist.pop():
            if r.dsem is not None:
                self.S.free_dma[r.dsem.kind].append(r.dsem)
                r.dsem = None

    def dres(self, name):
        return Res(name, dram=True)


def token_tiles(T, nt=NT):
    return [(t0, min(nt, T - t0)) for t0 in range(0, T, nt)]


def load_vec_fm(cx, v_dram, nch, name):
    t = cx.tile([128, nch], F32, name)
    cx.S.add("sp", lambda e: e.dma_start(out=t.t[:], in_=v_dram.rearrange("(c p) -> p c", p=128),
                                         allow_slow_non_contiguous=True), w=[t.res], dma=t.res)
    return t


def norm_to_resident(cx, h_dram, h_res, gamma, xn, T, kc=8, inv_n=1.0 / D):
    S = cx.S
    hv = h_dram.rearrange("(c p) t -> p c t", p=128)
    hts = cx.tiles(2, [128, kc, NT], F32, "nh")
    sqs = cx.tiles(2, [128, kc, NT], BF16, "nsq")
    rts = cx.tiles(2, [128, NT], F32, "nrs")
    for i, (t0, nt) in enumerate(token_tiles(T)):
        ht, sq, rt = hts[i % 2], sqs[i % 2], rts[i % 2]
        S.add("sp", lambda e: e.dma_start(out=ht.t[:, :, :nt], in_=hv[:, :, t0:t0 + nt]),
              r=[h_res], w=[ht.res], dma=ht.res)
        S.add("act", lambda e: e.activation(out=sq.t[:, :, :nt], in_=ht.t[:, :, :nt], func=AF.Square),
              r=[ht.res], w=[sq.res])
        ps = cx.psum()
        for c in range(kc):
            S.add("pe", lambda e: e.matmul(ps.t[:, :nt], lhsT=cx.ones.t[:], rhs=sq.t[:, c, :nt],
                                           start=(c == 0), stop=(c == kc - 1)),
                  r=[sq.res, cx.ones.res], w=[ps.res], sig=(c == kc - 1))
        S.add("act", lambda e: e.activation(out=rt.t[:, :nt], in_=ps.t[:, :nt], func=AF.Sqrt,
                                            bias=cx.eps.t[:], scale=inv_n), r=[ps.res, cx.eps.res], w=[rt.res])
        S.add("dve", lambda e: e.reciprocal(out=rt.t[:, :nt], in_=rt.t[:, :nt]), r=[rt.res], w=[rt.res])
        for c in range(kc):
            S.add("dve", lambda e: e.scalar_tensor_tensor(
                out=xn.t[:, c, t0:t0 + nt], in0=ht.t[:, c, :nt], scalar=gamma.t[:, c:c + 1],
                in1=rt.t[:, :nt], op0=ALU.mult, op1=ALU.mult),
                r=[ht.res, gamma.res, rt.res], w=[xn.res], sig=(c == kc - 1))


def gemm_resident(cx, xn, kc, W, groups, T, epi, epi_tile=None, nslots=3, wname="w"):
    S = cx.S
    gw = max(sum(w for _, w in g) for g in groups)
    slots = cx.tiles(nslots, [128, kc, gw], BF16, wname)
    npart = max(len(g) for g in groups)
    sparts = [cx.subres(npart, wname + "p") for _ in range(nslots)]
    Wv = W.rearrange("(c p) m -> p c m", p=128)

    def load(gi):
        sl = slots[gi % nslots]
        off = 0
        for pi, (c0, w) in enumerate(groups[gi]):
            o = off
            S.add("pool", lambda e: e.dma_start(out=sl.t[:, :, o:o + w], in_=Wv[:, :, c0:c0 + w]),
                  w=[sl.res], dma=sl.res)
            off += w

    load(0)
    for gi, g in enumerate(groups):
        if gi + 1 < len(groups):
            load(gi + 1)
        sl = slots[gi % nslots]
        for (t0, nt) in token_tiles(T):
            off = 0
            for ci, (c0, w) in enumerate(g):
                ps = cx.psum()
                for k in range(kc):
                    o = off
                    S.add("pe", lambda e: e.matmul(ps.t[:w, :nt], lhsT=sl.t[:, k, o:o + w],
                                                   rhs=xn.t[:, k, t0:t0 + nt],
                                                   start=(k == 0), stop=(k == kc - 1)),
                          r=[sl.res, xn.res], w=[ps.res], sig=(k == kc - 1))
                epi(gi, ci, ps, t0, nt)
                off += w
            if epi_tile is not None:
                epi_tile(gi, t0, nt)


def ffn_block(cx, h_in, h_in_res, h_out, h_out_res, gamma_d, w_gu, w_down, aT, aT_res, T):
    S = cx.S
    nj = DFF // 128
    if True:
     with cx.newphase():
        gamma = load_vec_fm(cx, gamma_d, 8, "gam")
        xn = cx.tile([128, 8, T], BF16, "xn")
        with cx.newphase():
            norm_to_resident(cx, h_in, h_in_res, gamma, xn, T)
        sgs = cx.tiles(2, [128, NT], F32, "sg")
        aos = cx.tiles(3, [128, NT], BF16, "ao")
        state = {"n": 0, "g": None}

        def epi(gi, ci, ps, t0, nt):
            if ci == 0:
                sg = sgs[state["n"] % 2]
                S.add("act", lambda e: e.activation(out=sg.t[:, :nt], in_=ps.t[:, :nt], func=AF.Silu),
                      r=[ps.res], w=[sg.res])
                state["g"] = sg
            else:
                sg = state["g"]
                ao = aos[state["n"] % 3]
                state["n"] += 1
                S.add("dve", lambda e: e.tensor_tensor(out=ao.t[:, :nt], in0=ps.t[:, :nt], in1=sg.t[:, :nt],
                                                       op=ALU.mult), r=[ps.res, sg.res], w=[ao.res])
                S.add("sp", lambda e: e.dma_start(out=aT[gi, :, t0:t0 + nt], in_=ao.t[:, :nt]),
                      r=[ao.res], w=[aT_res], dma=ao.res)

        groups = [[(j * 128, 128), (DFF + j * 128, 128)] for j in range(nj)]
        gemm_resident(cx, xn, 8, w_gu, groups, T, epi, wname="wgu")
    if True:
     with cx.newphase():
        wd = cx.tile([128, nj, D], BF16, "wd")
        wdv = w_down.rearrange("(c p) m -> p c m", p=128)
        wdp = cx.subres(nj, "wdp")
        for q in range(nj):
            S.add("pool", lambda e: e.dma_start(out=wd.t[:, q, :], in_=wdv[:, q, :]),
                  r=([wdp[q - 3]] if q >= 3 else []), w=[wdp[q]], dma=wdp[q])
        ats = cx.tiles(2, [128, nj, NT], BF16, "at")
        hts = cx.tiles(2, [128, 8, NT], F32, "dh")
        aTv = aT.rearrange("j p t -> p j t")
        hiv = h_in.rearrange("(c p) t -> p c t", p=128)
        hov = h_out.rearrange("(c p) t -> p c t", p=128)
        tl = token_tiles(T)

        def load(i):
            t0, nt = tl[i]
            at, ht = ats[i % 2], hts[i % 2]
            S.add("sp", lambda e: e.dma_start(out=at.t[:, :, :nt], in_=aTv[:, :, t0:t0 + nt]),
                  r=[aT_res], w=[at.res], dma=at.res)
            S.add("sp", lambda e: e.dma_start(out=ht.t[:, :, :nt], in_=hiv[:, :, t0:t0 + nt]),
                  r=[h_in_res], w=[ht.res], dma=ht.res)

        load(0)
        for i, (t0, nt) in enumerate(tl):
            if i + 1 < len(tl):
                load(i + 1)
            at, ht = ats[i % 2], hts[i % 2]
            for m in range(8):
                ps = cx.psum()
                for j in range(nj):
                    S.add("pe", lambda e: e.matmul(ps.t[:, :nt], lhsT=wd.t[:, j, m * 128:(m + 1) * 128],
                                                   rhs=at.t[:, j, :nt], start=(j == 0), stop=(j == nj - 1)),
                          r=[wdp[j], at.res], w=[ps.res], sig=(j == nj - 1))
                S.add("dve", lambda e: e.scalar_tensor_tensor(out=ht.t[:, m, :nt], in0=ps.t[:, :nt], scalar=0.5,
                                                              in1=ht.t[:, m, :nt], op0=ALU.mult, op1=ALU.add),
                      r=[ps.res, ht.res], w=[ht.res])
            S.add("sp", lambda e: e.dma_start(out=hov[:, :, t0:t0 + nt], in_=ht.t[:, :, :nt]),
                  r=[ht.res], w=[h_out_res], dma=ht.res)


NH = 8
TH = T_CORE
SC_A = 96.0 ** -0.5
SC_H = 128.0 ** -0.5
O_CQ, O_CKV, O_KPE, O_HQ, O_HF, O_HI, O_HG, O_GA, O_GB, O_KSW = 0, 384, 640, 672, 1184, 1696, 2208, 2720, 3744, 4768
DRAMR = Res("dram", dram=True)


def sub_norm(cx, src, kc, n, gam, dst, t0, nt, sq, rt):
    S = cx.S
    S.add("act", lambda e: e.activation(out=sq.t[:, :kc, :nt], in_=src.t[:, :kc, :nt], func=AF.Square),
          r=[src.res], w=[sq.res])
    ps = cx.psum()
    for c in range(kc):
        S.add("pe", lambda e: e.matmul(ps.t[:, :nt], lhsT=cx.ones.t[:], rhs=sq.t[:, c, :nt],
                                       start=(c == 0), stop=(c == kc - 1)),
              r=[sq.res, cx.ones.res], w=[ps.res], sig=(c == kc - 1))
    S.add("act", lambda e: e.activation(out=rt.t[:, :nt], in_=ps.t[:, :nt], func=AF.Sqrt,
                                        bias=cx.eps.t[:], scale=1.0 / n), r=[ps.res, cx.eps.res], w=[rt.res])
    S.add("dve", lambda e: e.reciprocal(out=rt.t[:, :nt], in_=rt.t[:, :nt]), r=[rt.res], w=[rt.res])
    for c in range(kc):
        S.add("dve", lambda e: e.scalar_tensor_tensor(
            out=dst.t[:, c, t0:t0 + nt], in0=src.t[:, c, :nt], scalar=gam.t[:, c:c + 1],
            in1=rt.t[:, :nt], op0=ALU.mult, op1=ALU.mult),
            r=[src.res, gam.res, rt.res], w=[dst.res], sig=(c == kc - 1))


def mix_pre(cx, lyr, Wl, G, th0):
    S = cx.S
    T = TH
    hT = G["hT"][:, th0:th0 + T]
    with cx.newphase():
        gamma = load_vec_fm(cx, Wl["mix_norm"], 8, "mg")
        qg = load_vec_fm(cx, Wl["q_norm"], 3, "qg")
        kg = load_vec_fm(cx, Wl["kv_norm"], 2, "kg")
        u = cx.tile([128, 8, T], BF16, "u")
        cqn = cx.tile([128, 3, T], BF16, "cqn")
        ckvn = cx.tile([128, 2, T], BF16, "ckvn")
        with cx.newphase():
            norm_to_resident(cx, hT, DRAMR, gamma, u, T)
        lbr = cx.tile([128, 4, 4], F32, "lbr")
        S.add("sp", lambda e: e.dma_start(out=lbr.t[:], in_=G["lb_raw"].rearrange("l (c p) -> p l c", p=128),
                                          allow_slow_non_contiguous=True), w=[lbr.res], dma=lbr.res)
        S.add("act", lambda e: e.activation(out=lbr.t[:], in_=lbr.t[:], func=AF.Exp), r=[lbr.res], w=[lbr.res])
        den = cx.tile([128, 4], F32, "lbden")
        lb = cx.tile([128, 4], F32, "lb")
        oml = cx.tile([128, 4], F32, "oml")
        S.add("dve", lambda e: e.tensor_tensor(out=den.t[:], in0=lbr.t[:, 0, :], in1=lbr.t[:, 1, :], op=ALU.add),
              r=[lbr.res], w=[den.res])
        for l in (2, 3):
            S.add("dve", lambda e: e.tensor_tensor(out=den.t[:], in0=den.t[:], in1=lbr.t[:, l, :], op=ALU.add),
                  r=[lbr.res, den.res], w=[den.res])
        S.add("dve", lambda e: e.reciprocal(out=den.t[:], in_=den.t[:]), r=[den.res], w=[den.res])
        S.add("dve", lambda e: e.memset(lb.t[:], 0.0), w=[lb.res])
        for l in range(1, lyr + 1):
            S.add("dve", lambda e: e.tensor_tensor(out=lb.t[:], in0=lb.t[:], in1=lbr.t[:, l, :], op=ALU.add),
                  r=[lbr.res, lb.res], w=[lb.res])
        S.add("dve", lambda e: e.tensor_tensor(out=lb.t[:], in0=lb.t[:], in1=den.t[:], op=ALU.mult),
              r=[lb.res, den.res], w=[lb.res])
        S.add("dve", lambda e: e.tensor_scalar(out=oml.t[:], in0=lb.t[:], scalar1=-1.0, scalar2=1.0,
                                               op0=ALU.mult, op1=ALU.add), r=[lb.res], w=[oml.res])

        raws = cx.tiles(2, [128, 3, NT], F32, "raw")
        sqs = cx.tiles(2, [128, 3, NT], BF16, "sq3")
        rts = cx.tiles(2, [128, NT], F32, "rt3")
        obs = cx.tiles(4, [128, NT], BF16, "ob")
        ofs = cx.tiles(2, [128, NT], F32, "of")
        fts = cx.tiles(2, [128, NT], F32, "ft")
        rps = cx.tiles(2, [128, 2, NT], F32, "rope")
        st = {"n": 0, "a": None}
        ropev = G["rope"]

        def out_bf(ps, rows, func, dst, t0, nt, scale=1.0):
            ob = obs[st["n"] % 4]
            st["n"] += 1
            S.add("act", lambda e: e.activation(out=ob.t[:rows, :nt], in_=ps.t[:rows, :nt], func=func, scale=scale),
                  r=[ps.res], w=[ob.res])
            store(ob, dst, rows, t0, nt)

        def store(ob, dst, rows, t0, nt):
            pieces = dst if isinstance(dst, list) else [(0, rows, dst)]
            for (r0, nr, d) in pieces:
                S.add("sp", lambda e: e.dma_start(out=d[:, th0 + t0:th0 + t0 + nt], in_=ob.t[r0:r0 + nr, :nt]),
                      r=[ob.res], dma=ob.res)

        def rope_out(psa, psb, rows, dst, t0, nt):
            rp = rps[st["n"] % 2]
            of = ofs[st["n"] % 2]
            ob = obs[st["n"] % 4]
            st["n"] += 1
            S.add("sp", lambda e: e.dma_start(out=rp.t[:rows, :, :nt], in_=ropev[:rows, :, th0 + t0:th0 + t0 + nt]),
                  w=[rp.res], dma=rp.res)
            S.add("dve", lambda e: e.tensor_tensor(out=of.t[:rows, :nt], in0=psa.t[:rows, :nt], in1=rp.t[:rows, 0, :nt],
                                                   op=ALU.mult), r=[psa.res, rp.res], w=[of.res])
            S.add("dve", lambda e: e.tensor_tensor(out=rp.t[:rows, 1, :nt], in0=psb.t[:rows, :nt], in1=rp.t[:rows, 1, :nt],
                                                   op=ALU.mult), r=[psb.res, rp.res], w=[rp.res])
            S.add("dve", lambda e: e.tensor_tensor(out=ob.t[:rows, :nt], in0=of.t[:rows, :nt], in1=rp.t[:rows, 1, :nt],
                                                   op=ALU.add), r=[of.res, rp.res], w=[ob.res])
            store(ob, dst, rows, t0, nt)

        groups, kinds = [], []
        groups.append([(O_CQ + 128 * i, 128) for i in range(3)]); kinds.append(("cq",))
        groups.append([(O_CKV + 128 * i, 128) for i in range(2)]); kinds.append(("ckv",))
        groups.append([(O_KPE, 32), (O_KSW, 32)]); kinds.append(("kpe",))
        for i in range(4):
            groups.append([(O_HQ + 128 * i, 128), (O_HG + 128 * i, 128)]); kinds.append(("hqg", i))
        for i in range(4):
            groups.append([(O_HF + 128 * i, 128)]); kinds.append(("hf", i))
        for i in range(8):
            groups.append([(O_GA + 128 * i, 128), (O_GB + 128 * i, 128)]); kinds.append(("gab", i))

        def epi(gi, ci, ps, t0, nt):
            kd = kinds[gi]
            if kd[0] in ("cq", "ckv"):
                raw = raws[(t0 // NT) % 2]
                S.add("act", lambda e: e.copy(out=raw.t[:, ci, :nt], in_=ps.t[:, :nt]), r=[ps.res], w=[raw.res])
            elif kd[0] == "kpe":
                if ci == 0:
                    st["a"] = ps
                else:
                    rope_out(st["a"], ps, 32, G["kpeT"], t0, nt)
            elif kd[0] == "hqg":
                i = kd[1]
                dst = G["hqT"] if ci == 0 else G["hgT"]
                out_bf(ps, 128, AF.Silu, dst[i * 128:(i + 1) * 128, :], t0, nt)
            elif kd[0] == "gab":
                i = kd[1]
                dst = G["gaT"] if ci == 0 else G["gbT"]
                out_bf(ps, 128, AF.Sigmoid, dst[i * 128:(i + 1) * 128, :], t0, nt)
            elif kd[0] == "hf":
                i = kd[1]
                ft = fts[st["n"] % 2]
                of = ofs[st["n"] % 2]
                ob = obs[st["n"] % 4]
                st["n"] += 1
                S.add("act", lambda e: e.activation(out=ft.t[:, :nt], in_=ps.t[:, :nt], func=AF.Sigmoid),
                      r=[ps.res], w=[ft.res])
                S.add("dve", lambda e: e.tensor_scalar(out=ft.t[:, :nt], in0=ft.t[:, :nt], scalar1=oml.t[:, i:i + 1],
                                                       scalar2=lb.t[:, i:i + 1], op0=ALU.mult, op1=ALU.add),
                      r=[ft.res, oml.res, lb.res], w=[ft.res])
                S.add("dve", lambda e: e.tensor_scalar(out=ft.t[:, :nt], in0=ft.t[:, :nt], scalar1=1e-20, scalar2=None,
                                                       op0=ALU.max), r=[ft.res], w=[ft.res])
                S.add("act", lambda e: e.activation(out=of.t[:, :nt], in_=ft.t[:, :nt], func=AF.Ln),
                      r=[ft.res], w=[of.res])
                S.add("sp", lambda e: e.dma_start(out=G["logfT"][i * 128:(i + 1) * 128, th0 + t0:th0 + t0 + nt],
                                                  in_=of.t[:, :nt]), r=[of.res], dma=of.res)
                S.add("dve", lambda e: e.tensor_scalar(out=ob.t[:, :nt], in0=ft.t[:, :nt], scalar1=-1.0, scalar2=1.0,
                                                       op0=ALU.mult, op1=ALU.add), r=[ft.res], w=[ob.res])
                S.add("sp", lambda e: e.dma_start(out=G["kkT"][i * 128:(i + 1) * 128, th0 + t0:th0 + t0 + nt],
                                                  in_=ob.t[:, :nt]), r=[ob.res], dma=ob.res)

        def epi_tile(gi, t0, nt):
            kd = kinds[gi]
            i = (t0 // NT) % 2
            if kd[0] == "cq":
                sub_norm(cx, raws[i], 3, 384.0, qg, cqn, t0, nt, sqs[i], rts[i])
            elif kd[0] == "ckv":
                sub_norm(cx, raws[i], 2, 256.0, kg, ckvn, t0, nt, sqs[i], rts[i])

        gemm_resident(cx, u, 8, Wl["w_in"], groups, T, epi, epi_tile, wname="win")

        whi = cx.tile([128, 8, 512], BF16, "whi")
        wv = cx.tile([128, 2, 512], BF16, "wv")
        winv = Wl["w_in"].rearrange("(c p) m -> p c m", p=128)
        whip, wvp = cx.subres(8, "whip"), cx.subres(2, "wvp")
        for c in range(8):
            S.add("pool", lambda e: e.dma_start(out=whi.t[:, c, :], in_=winv[:, c, O_HI:O_HI + 512]),
                  r=([whip[c - 3]] if c >= 3 else []), w=[whip[c]], dma=whip[c])
        wkv = Wl["w_ukv"].rearrange("(c p) m -> p c m", p=128)
        for c in range(2):
            S.add("pool", lambda e: e.dma_start(out=wv.t[:, c, :], in_=wkv[:, c, 512:1024]), w=[wvp[c]], dma=wvp[c])
        tms = cx.tiles(3, [128, 512], BF16, "tm")
        k = 0
        for b0 in range(0, T, 128):
            nb = min(128, T - b0)
            for (src, kc, wt, wtp, dst) in ((u, 8, whi, whip, G["hiTM"]), (ckvn, 2, wv, wvp, G["vTM"])):
                ps = cx.psum()
                for c in range(kc):
                    S.add("pe", lambda e: e.matmul(ps.t[:nb, :], lhsT=src.t[:, c, b0:b0 + nb], rhs=wt.t[:, c, :],
                                                   start=(c == 0), stop=(c == kc - 1)),
                          r=[src.res, wtp[c]], w=[ps.res], sig=(c == kc - 1))
                tm = tms[k % 3]
                k += 1
                S.add("act" if k % 2 else "dve",
                      (lambda e: e.copy(out=tm.t[:nb, :], in_=ps.t[:nb, :])) if k % 2 else
                      (lambda e: e.tensor_copy(out=tm.t[:nb, :], in_=ps.t[:nb, :])), r=[ps.res], w=[tm.res])
                S.add("sp", lambda e: e.dma_start(out=dst[th0 + b0:th0 + b0 + nb, :], in_=tm.t[:nb, :]),
                      r=[tm.res], dma=tm.res)

        qT, kT = G["qT"], G["kT"]
        gq = [[(128 * c, 128)] for c in range(4)] + [[(512 + 128 * c, 128), (768 + 128 * c, 128)] for c in range(2)]

        def epi_q(gi, ci, ps, t0, nt):
            if gi < 4:
                out_bf(ps, 128, AF.Copy, [(64 * j, 64, qT[2 * gi + j, 0:64, :]) for j in range(2)], t0, nt)
            elif ci == 0:
                st["a"] = ps
            else:
                c = gi - 4
                rope_out(st["a"], ps, 128, [(32 * j, 32, qT[4 * c + j, 64:96, :]) for j in range(4)], t0, nt)

        gemm_resident(cx, cqn, 3, Wl["w_uq"], gq, T, epi_q, wname="wuq")

        def epi_k(gi, ci, ps, t0, nt):
            out_bf(ps, 128, AF.Copy, [(64 * j, 64, kT[2 * gi + j, 0:64, :]) for j in range(2)], t0, nt)

        gemm_resident(cx, ckvn, 2, Wl["w_ukv"], [[(128 * c, 128)] for c in range(4)], T, epi_k, wname="wuk")


LP = 8320
NKT = LP // 128


def attention_gen(cx, G, C):
    S = cx.S
    L = L_TOT
    KT = cx.tiles(2, [96, LP], BF16, "KT")
    QT = cx.tiles(2, [96, LP], BF16, "QT")
    V = cx.tiles(2, [128, NKT, 65], BF16, "V")
    pts = cx.tiles(5, [128, 512], BF16, "pt")
    osb = cx.tiles(2, [65, 512], F32, "osb")
    rds = cx.tiles(2, [64, 512], F32, "rd")
    oos = cx.tiles(2, [64, 512], BF16, "oo")
    sel = cx.tile([65, 64], F32, "sel")
    S.add("pool", lambda e: e.memset(sel.t[:], 0.0), w=[sel.res])
    S.add("pool", lambda e: e.memset(sel.t[64:65, :], 1.0), w=[sel.res])
    for i in range(2):
        S.add("pool", lambda e: e.memset(KT[i].t[:, L:LP], 0.0), w=[KT[i].res])
        S.add("pool", lambda e: e.memset(QT[i].t[:, L:LP], 0.0), w=[QT[i].res])
        S.add("pool", lambda e: e.memset(V[i].t[:], 0.0), w=[V[i].res])
        S.add("pool", lambda e: e.memset(V[i].t[:, :, 64:65], 1.0), w=[V[i].res])
    vv = G["vTM"]

    def load(h):
        kt, qt, v = KT[h % 2], QT[h % 2], V[h % 2]
        S.add("sp", lambda e: e.dma_start(out=kt.t[0:64, 0:L], in_=G["kT"][h, 0:64, :]), w=[kt.res], dma=kt.res)
        S.add("sp", lambda e: e.dma_start(out=kt.t[64:96, 0:L], in_=G["kpeT"][:, :]), w=[kt.res], dma=kt.res)
        S.add("sp", lambda e: e.dma_start(out=qt.t[:, 0:L], in_=G["qT"][h, :, :]), w=[qt.res], dma=qt.res)
        nf = L // 128
        S.add("sp", lambda e: e.dma_start(
            out=v.t[:, 0:nf, 0:64], in_=vv[0:nf * 128, h * 64:(h + 1) * 64].rearrange("(n p) d -> p n d", p=128)),
            w=[v.res], dma=v.res)
        rem = L - nf * 128
        S.add("sp", lambda e: e.dma_start(out=v.t[0:rem, nf, 0:64], in_=vv[nf * 128:L, h * 64:(h + 1) * 64]),
              w=[v.res], dma=v.res)

    load(0)
    yield
    st = {"n": 0, "q": 0}
    DEPTH_P = 2
    for h in range(NH):
        if h + 1 < NH:
            load(h + 1)
        kt, qt, v = KT[h % 2], QT[h % 2], V[h % 2]
        for q0 in range(0, LP, 512):
            w = min(512, LP - q0)
            nk = (q0 + w) // 128
            acc = cx.psum_acc()
            ptl = {}

            def score(i):
                ps = cx.psum()
                S.add("pe", lambda e: e.matmul(ps.t[:, :w], lhsT=kt.t[:, i * 128:(i + 1) * 128], rhs=qt.t[:, q0:q0 + w],
                                               start=True, stop=True), r=[kt.res, qt.res], w=[ps.res])
                pt = pts[st["n"] % 5]
                st["n"] += 1
                S.add("act", lambda e: e.activation(out=pt.t[:, :w], in_=ps.t[:, :w], func=AF.Exp, scale=SC_A),
                      r=[ps.res], w=[pt.res])
                d = i - q0 // 128
                if d >= 0:
                    S.add("pool", lambda e: e.tensor_tensor(out=pt.t[:, :w], in0=pt.t[:, :w], in1=C["amask"].t[:, d, :w],
                                                            op=ALU.mult), r=[pt.res, C["amask"].res], w=[pt.res])
                ptl[i] = pt

            for i in range(min(DEPTH_P, nk)):
                score(i)
            for i in range(nk):
                if i + DEPTH_P < nk:
                    score(i + DEPTH_P)
                pt = ptl.pop(i)
                S.add("pe", lambda e: e.matmul(acc.t[0:65, :w], lhsT=v.t[:, i, :], rhs=pt.t[:, :w],
                                               start=(i == 0), stop=(i == nk - 1)),
                      r=[v.res, pt.res], w=[acc.res], sig=(i == nk - 1))
                yield
            k = st["q"] % 2
            st["q"] += 1
            ob, rd, oo = osb[k], rds[k], oos[k]
            S.add("dve", lambda e: e.tensor_copy(out=ob.t[:, :w], in_=acc.t[0:65, :w]), r=[acc.res], w=[ob.res])
            ps = cx.psum()
            S.add("pe", lambda e: e.matmul(ps.t[0:64, :w], lhsT=sel.t[:, :], rhs=ob.t[:, :w], start=True, stop=True),
                  r=[sel.res, ob.res], w=[ps.res])
            S.add("dve", lambda e: e.reciprocal(out=rd.t[:, :w], in_=ps.t[0:64, :w]), r=[ps.res], w=[rd.res])
            S.add("dve", lambda e: e.tensor_tensor(out=oo.t[:, :w], in0=ob.t[0:64, :w], in1=rd.t[:, :w], op=ALU.mult),
                  r=[ob.res, rd.res], w=[oo.res])
            wr = min(w, L - q0)
            S.add("sp", lambda e: e.dma_start(out=G["oaT"][h * 64:(h + 1) * 64, q0:q0 + wr], in_=oo.t[:, :wr]),
                  r=[oo.res], dma=oo.res)
            yield


def hgrn_gen(cx, G, C, Wl):
    S = cx.S
    L = L_TOT
    CW = 64
    nw = cx.tile([128, 1], F32, "hgnw")
    S.add("sp", lambda e: e.dma_start(out=nw.t[:], in_=Wl["hg_norm"].rearrange("(p o) -> p o", o=1)),
          w=[nw.res], dma=nw.res)
    Sf = cx.tile([128, 128], F32, "Sf")
    Sb = cx.tile([128, 128], BF16, "Sb")
    qs = cx.tiles(2, [128, 512], BF16, "hq")
    lfs = cx.tiles(2, [128, 512], F32, "hlf")
    kks = cx.tiles(2, [128, 512], BF16, "hkk")
    gs = cx.tiles(2, [128, 512], BF16, "hgg")
    vs = cx.tiles(2, [64, 8, 128], BF16, "hv")
    bt = cx.tile([128, 512], F32, "hb")
    d1 = cx.tile([128, 512], F32, "hd1")
    e1 = cx.tile([128, 512], F32, "he1")
    e2 = cx.tile([128, 512], F32, "he2")
    qtl = cx.tile([128, 512], BF16, "hqt")
    ktl = cx.tile([128, 512], BF16, "hkt")
    qil = cx.tile([128, 512], BF16, "hqi")
    ksl = cx.tile([128, 512], BF16, "hks")
    dc = cx.tile([128, 8], F32, "hdc")
    kst = cx.tiles(2, [64, 128], BF16, "hkst")
    am = cx.tiles(2, [64, 64], BF16, "ham")
    sq = cx.tile([128, 512], BF16, "hsq")
    rt = cx.tile([128, 512], F32, "hrt")
    on = cx.tile([128, 512], F32, "hon")
    ou = cx.tiles(2, [128, 512], BF16, "hou")
    segs = [(s0, min(512, L - s0)) for s0 in range(0, L, 512)]
    acc = cx.ps[5]

    def load(hd, si):
        s0, w = segs[si]
        k = si % 2
        r = slice(hd * 128, (hd + 1) * 128)
        S.add("sp", lambda e: e.dma_start(out=qs[k].t[:, :w], in_=G["hqT"][r, s0:s0 + w]), w=[qs[k].res], dma=qs[k].res)
        S.add("sp", lambda e: e.dma_start(out=lfs[k].t[:, :w], in_=G["logfT"][r, s0:s0 + w]), w=[lfs[k].res], dma=lfs[k].res)
        S.add("sp", lambda e: e.dma_start(out=kks[k].t[:, :w], in_=G["kkT"][r, s0:s0 + w]), w=[kks[k].res], dma=kks[k].res)
        S.add("sp", lambda e: e.dma_start(out=gs[k].t[:, :w], in_=G["hgT"][r, s0:s0 + w]), w=[gs[k].res], dma=gs[k].res)
        cw = min(CW, w)
        nc_ = w // cw
        S.add("sp", lambda e: e.dma_start(
            out=vs[k].t[:cw, :nc_, :], in_=G["hiTM"][s0:s0 + w, r].rearrange("(n s) v -> s n v", s=cw)),
            w=[vs[k].res], dma=vs[k].res)

    yield
    for hd in range(4):
        S.add("dve", lambda e: e.memset(Sf.t[:], 0.0), w=[Sf.res])
        S.add("dve", lambda e: e.memset(Sb.t[:], 0.0), w=[Sb.res])
        load(hd, 0)
        for si, (s0, w) in enumerate(segs):
            if si + 1 < len(segs):
                load(hd, si + 1)
            k = si % 2
            q, lf, kk, g, v = qs[k], lfs[k], kks[k], gs[k], vs[k]
            cw = min(CW, w)
            nch = w // cw
            mid = cw // 2 - 1
            b3 = bt.t[:, :w].rearrange("p (n s) -> p n s", s=cw)
            d3 = d1.t[:, :w].rearrange("p (n s) -> p n s", s=cw)
            S.add("dve", lambda e: e.tensor_tensor_scan(out=bt.t[:, :w], data0=C["cm"].t[:, :w], data1=lf.t[:, :w],
                                                        initial=0.0, op0=ALU.mult, op1=ALU.add),
                  r=[C["cm"].res, lf.res], w=[bt.res])
            S.add("dve", lambda e: e.tensor_tensor(out=d3, in0=b3, in1=b3[:, :, mid:mid + 1].to_broadcast([128, nch, cw]),
                                                   op=ALU.subtract), r=[bt.res], w=[d1.res])
            S.add("dve", lambda e: e.tensor_scalar(out=d1.t[:, :w], in0=d1.t[:, :w], scalar1=-43.0, scalar2=43.0,
                                                   op0=ALU.max, op1=ALU.min), r=[d1.res], w=[d1.res])
            S.add("act", lambda e: e.activation(out=e1.t[:, :w], in_=d1.t[:, :w], func=AF.Exp), r=[d1.res], w=[e1.res])
            S.add("act", lambda e: e.activation(out=e2.t[:, :w], in_=d1.t[:, :w], func=AF.Exp, scale=-1.0),
                  r=[d1.res], w=[e2.res])
            S.add("dve", lambda e: e.scalar_tensor_tensor(out=qtl.t[:, :w], in0=e1.t[:, :w], scalar=SC_H, in1=q.t[:, :w],
                                                          op0=ALU.mult, op1=ALU.mult), r=[e1.res, q.res], w=[qtl.res])
            S.add("pool", lambda e: e.tensor_tensor(out=ktl.t[:, :w], in0=e2.t[:, :w], in1=kk.t[:, :w], op=ALU.mult),
                  r=[e2.res, kk.res], w=[ktl.res])
            yield
            S.add("act", lambda e: e.activation(out=e1.t[:, :w], in_=bt.t[:, :w], func=AF.Exp), r=[bt.res], w=[e1.res])
            S.add("dve", lambda e: e.scalar_tensor_tensor(out=qil.t[:, :w], in0=e1.t[:, :w], scalar=SC_H, in1=q.t[:, :w],
                                                          op0=ALU.mult, op1=ALU.mult), r=[e1.res, q.res], w=[qil.res])
            S.add("dve", lambda e: e.tensor_tensor(out=d3, in0=b3, in1=b3[:, :, cw - 1:cw].to_broadcast([128, nch, cw]),
                                                   op=ALU.subtract), r=[bt.res], w=[d1.res])
            S.add("act", lambda e: e.activation(out=e2.t[:, :w], in_=d1.t[:, :w], func=AF.Exp, scale=-1.0),
                  r=[d1.res], w=[e2.res])
            S.add("pool", lambda e: e.tensor_tensor(out=ksl.t[:, :w], in0=e2.t[:, :w], in1=kk.t[:, :w], op=ALU.mult),
                  r=[e2.res, kk.res], w=[ksl.res])
            S.add("act", lambda e: e.activation(out=dc.t[:, :nch], in_=b3[:, :, cw - 1], func=AF.Exp),
                  r=[bt.res], w=[dc.res])
            yield
            for n in range(nch):
                c0 = n * cw
                ps = cx.psum()
                S.add("pe", lambda e: e.matmul(ps.t[:cw, 0:128], lhsT=ksl.t[:, c0:c0 + cw], rhs=C["ident"].t[:, :],
                                               start=True, stop=True), r=[ksl.res, C["ident"].res], w=[ps.res])
                ks_ = kst[n % 2]
                S.add("dve", lambda e: e.tensor_copy(out=ks_.t[:cw, :], in_=ps.t[:cw, 0:128]), r=[ps.res], w=[ks_.res])
                ps2 = cx.psum()
                S.add("pe", lambda e: e.matmul(ps2.t[:cw, :cw], lhsT=ktl.t[:, c0:c0 + cw], rhs=qtl.t[:, c0:c0 + cw],
                                               start=True, stop=True), r=[ktl.res, qtl.res], w=[ps2.res])
                am_ = am[n % 2]
                S.add("dve", lambda e: e.tensor_tensor(out=am_.t[:cw, :cw], in0=ps2.t[:cw, :cw], in1=C["hmask"].t[:cw, :cw],
                                                       op=ALU.mult), r=[ps2.res, C["hmask"].res], w=[am_.res])
                yield
                S.add("pe", lambda e: e.matmul(acc.t[:, c0:c0 + cw], lhsT=v.t[:cw, n, :], rhs=am_.t[:cw, :cw],
                                               start=True, stop=False), r=[v.res, am_.res], w=[acc.res], sig=False)
                S.add("pe", lambda e: e.matmul(acc.t[:, c0:c0 + cw], lhsT=Sb.t[:, :], rhs=qil.t[:, c0:c0 + cw],
                                               start=False, stop=True), r=[Sb.res, qil.res], w=[acc.res])
                ps3 = cx.psum()
                S.add("pe", lambda e: e.matmul(ps3.t[:, 0:128], lhsT=ks_.t[:cw, :], rhs=v.t[:cw, n, :],
                                               start=True, stop=True), r=[ks_.res, v.res], w=[ps3.res])
                S.add("dve", lambda e: e.scalar_tensor_tensor(out=Sf.t[:], in0=Sf.t[:], scalar=dc.t[:, n:n + 1],
                                                              in1=ps3.t[:, 0:128], op0=ALU.mult, op1=ALU.add),
                      r=[Sf.res, dc.res, ps3.res], w=[Sf.res])
                S.add("pool", lambda e: e.tensor_copy(out=Sb.t[:], in_=Sf.t[:]), r=[Sf.res], w=[Sb.res])
                yield
            S.add("act", lambda e: e.activation(out=sq.t[:, :w], in_=acc.t[:, :w], func=AF.Square), r=[acc.res], w=[sq.res])
            ps = cx.psum()
            S.add("pe", lambda e: e.matmul(ps.t[:, :w], lhsT=cx.ones.t[:], rhs=sq.t[:, :w], start=True, stop=True),
                  r=[sq.res, cx.ones.res], w=[ps.res])
            S.add("dve", lambda e: e.tensor_scalar(out=on.t[:, :w], in0=acc.t[:, :w], scalar1=nw.t[:, 0:1], scalar2=None,
                                                   op0=ALU.mult), r=[acc.res, nw.res, sq.res], w=[on.res])
            S.add("act", lambda e: e.activation(out=rt.t[:, :w], in_=ps.t[:, :w], func=AF.Sqrt, bias=cx.eps.t[:],
                                                scale=1.0 / 128), r=[ps.res, cx.eps.res], w=[rt.res])
            yield
            S.add("dve", lambda e: e.reciprocal(out=rt.t[:, :w], in_=rt.t[:, :w]), r=[rt.res], w=[rt.res])
            S.add("pool", lambda e: e.tensor_tensor(out=on.t[:, :w], in0=on.t[:, :w], in1=g.t[:, :w], op=ALU.mult),
                  r=[on.res, g.res], w=[on.res])
            o_ = ou[si % 2]
            S.add("pool", lambda e: e.tensor_tensor(out=o_.t[:, :w], in0=on.t[:, :w], in1=rt.t[:, :w], op=ALU.mult),
                  r=[on.res, rt.res], w=[o_.res])
            S.add("sp", lambda e: e.dma_start(out=G["orT"][hd * 128:(hd + 1) * 128, s0:s0 + w], in_=o_.t[:, :w]),
                  r=[o_.res], dma=o_.res)
            yield


def mixers(cx, G, C, Wl, ratio=6):
    import os
    ratio = int(os.environ.get("MIX_RATIO", ratio))
    with cx.newphase():
        ga = attention_gen(cx, G, C)
        gh = hgrn_gen(cx, G, C, Wl)
        next(ga)
        next(gh)
        k = 0
        alive = True
        for _ in ga:
            k += 1
            if alive and k % ratio == 0:
                try:
                    next(gh)
                except StopIteration:
                    alive = False
        if alive:
            for _ in gh:
                pass


def mix_post(cx, Wl, G, th0):
    S = cx.S
    T = TH
    with cx.newphase():
        wa = cx.tile([128, 4, D], BF16, "wa")
        wr = cx.tile([128, 4, D], BF16, "wr")
        wo = cx.tile([128, 8, D], BF16, "wo")
        wparts = {}
        for (wt, src, kc) in ((wa, Wl["w_proj_attn"], 4), (wr, Wl["w_proj_rec"], 4), (wo, Wl["w_out"], 8)):
            sv = src.rearrange("(c p) m -> p c m", p=128)
            wparts[id(wt)] = cx.subres(kc, "wpp")
            for c in range(kc):
                pr = wparts[id(wt)][c]
                S.add("pool", lambda e: e.dma_start(out=wt.t[:, c, :], in_=sv[:, c, :]),
                      r=([wparts[id(wt)][c - 3]] if c >= 3 else []), w=[pr], dma=pr)
        oas = cx.tiles(2, [128, 4, NT], BF16, "poa")
        ors = cx.tiles(2, [128, 4, NT], BF16, "por")
        gas = cx.tiles(2, [128, 8, NT], BF16, "pga")
        gbs = cx.tiles(2, [128, 8, NT], BF16, "pgb")
        hts = cx.tiles(2, [128, 8, NT], F32, "ph")
        mgs = cx.tiles(2, [128, 8, NT], BF16, "pmg")
        t1s = cx.tiles(2, [128, NT], F32, "pt1")
        t2s = cx.tiles(2, [128, NT], F32, "pt2")
        hv = G["hT"][:, th0:th0 + T].rearrange("(c p) t -> p c t", p=128)
        tl = token_tiles(T)

        def fm(name, t0, nt):
            return G[name][:, th0 + t0:th0 + t0 + nt].rearrange("(c p) t -> p c t", p=128)

        def load(i):
            t0, nt = tl[i]
            k = i % 2
            for (tt, name) in ((oas[k], "oaT"), (ors[k], "orT"), (gas[k], "gaT"), (gbs[k], "gbT")):
                S.add("sp", lambda e: e.dma_start(out=tt.t[:, :, :nt], in_=fm(name, t0, nt)), w=[tt.res], dma=tt.res)
            S.add("sp", lambda e: e.dma_start(out=hts[k].t[:, :, :nt], in_=hv[:, :, t0:t0 + nt]), w=[hts[k].res], dma=hts[k].res)

        load(0)
        n = 0
        for i, (t0, nt) in enumerate(tl):
            if i + 1 < len(tl):
                load(i + 1)
            k = i % 2
            oa, orr, ga, gb, ht, mg = oas[k], ors[k], gas[k], gbs[k], hts[k], mgs[k]
            for c in range(8):
                psa, psb = cx.psum(), cx.psum()
                for (ps, wt, xx) in ((psa, wa, oa), (psb, wr, orr)):
                    for kk in range(4):
                        S.add("pe", lambda e: e.matmul(ps.t[:, :nt], lhsT=wt.t[:, kk, c * 128:(c + 1) * 128], rhs=xx.t[:, kk, :nt],
                                                       start=(kk == 0), stop=(kk == 3)), r=[wparts[id(wt)][kk], xx.res], w=[ps.res], sig=(kk == 3))
                t1, t2 = t1s[n % 2], t2s[n % 2]
                n += 1
                S.add("dve", lambda e: e.tensor_tensor(out=t1.t[:, :nt], in0=psa.t[:, :nt], in1=ga.t[:, c, :nt], op=ALU.mult),
                      r=[psa.res, ga.res], w=[t1.res])
                S.add("dve", lambda e: e.tensor_tensor(out=t2.t[:, :nt], in0=psb.t[:, :nt], in1=gb.t[:, c, :nt], op=ALU.mult),
                      r=[psb.res, gb.res], w=[t2.res])
                S.add("pool", lambda e: e.tensor_tensor(out=mg.t[:, c, :nt], in0=t1.t[:, :nt], in1=t2.t[:, :nt], op=ALU.add),
                      r=[t1.res, t2.res], w=[mg.res])
            for m in range(8):
                ps = cx.psum()
                for kk in range(8):
                    S.add("pe", lambda e: e.matmul(ps.t[:, :nt], lhsT=wo.t[:, kk, m * 128:(m + 1) * 128], rhs=mg.t[:, kk, :nt],
                                                   start=(kk == 0), stop=(kk == 7)), r=[wparts[id(wo)][kk], mg.res], w=[ps.res], sig=(kk == 7))
                S.add("dve", lambda e: e.tensor_tensor(out=ht.t[:, m, :nt], in0=ps.t[:, :nt], in1=ht.t[:, m, :nt], op=ALU.add),
                      r=[ps.res, ht.res], w=[ht.res])
            S.add("sp", lambda e: e.dma_start(out=hv[:, :, t0:t0 + nt], in_=ht.t[:, :, :nt]), r=[ht.res], dma=ht.res)


def final_norm(cx, G, gamma_d, yT, th0):
    S = cx.S
    T = TH
    with cx.newphase():
        gamma = load_vec_fm(cx, gamma_d, 8, "fg")
        hv = G["hT"][:, th0:th0 + T].rearrange("(c p) t -> p c t", p=128)
        yv = yT.rearrange("(c p) t -> p c t", p=128)
        hts = cx.tiles(2, [128, 8, NT], F32, "fh")
        sqs = cx.tiles(2, [128, 8, NT], BF16, "fsq")
        rts = cx.tiles(2, [128, NT], F32, "frs")
        for i, (t0, nt) in enumerate(token_tiles(T)):
            ht, sq, rt = hts[i % 2], sqs[i % 2], rts[i % 2]
            S.add("sp", lambda e: e.dma_start(out=ht.t[:, :, :nt], in_=hv[:, :, t0:t0 + nt]), w=[ht.res], dma=ht.res)
            S.add("act", lambda e: e.activation(out=sq.t[:, :, :nt], in_=ht.t[:, :, :nt], func=AF.Square), r=[ht.res], w=[sq.res])
            ps = cx.psum()
            for c in range(8):
                S.add("pe", lambda e: e.matmul(ps.t[:, :nt], lhsT=cx.ones.t[:], rhs=sq.t[:, c, :nt], start=(c == 0), stop=(c == 7)),
                      r=[sq.res, cx.ones.res], w=[ps.res], sig=(c == 7))
            S.add("act", lambda e: e.activation(out=rt.t[:, :nt], in_=ps.t[:, :nt], func=AF.Sqrt, bias=cx.eps.t[:], scale=1.0 / D),
                  r=[ps.res, cx.eps.res], w=[rt.res])
            S.add("dve", lambda e: e.reciprocal(out=rt.t[:, :nt], in_=rt.t[:, :nt]), r=[rt.res], w=[rt.res])
            for c in range(8):
                S.add("dve", lambda e: e.scalar_tensor_tensor(out=ht.t[:, c, :nt], in0=ht.t[:, c, :nt], scalar=gamma.t[:, c:c + 1],
                                                              in1=rt.t[:, :nt], op0=ALU.mult, op1=ALU.mult),
                      r=[ht.res, gamma.res, rt.res], w=[ht.res], sig=(c == 7))
            g0 = th0 + t0 - 16
            skip = max(0, -g0)
            if nt - skip > 0:
                S.add("sp", lambda e: e.dma_start(out=yv[:, :, g0 + skip:g0 + nt], in_=ht.t[:, :, skip:nt]), r=[ht.res], dma=ht.res)


WNAMES = {"ffn1_norm": [D], "ffn1_w_gu": [D, 2 * DFF], "ffn1_w_down": [DFF, D], "mix_norm": [D],
          "w_in": [D, 4800], "q_norm": [384], "kv_norm": [256], "w_uq": [384, 1024], "w_ukv": [256, 1024],
          "hg_norm": [128], "w_proj_attn": [512, D], "w_proj_rec": [512, D], "w_out": [D, D],
          "ffn2_norm": [D], "ffn2_w_gu": [D, 2 * DFF], "ffn2_w_down": [DFF, D]}


def build_program(nl=4):
    nc = bass.Bass("TRN2", target_bir_lowering=False)
    L = L_TOT

    def din(name, shape, dt=F32):
        return nc.dram_tensor(name, list(shape), dt, kind="ExternalInput").ap()

    def dsc(name, shape, dt):
        return nc.dram_tensor(name, list(shape), dt, kind="Internal").ap()

    xT = din("xT", [D, L])
    Wd = {k: din(k, [nl] + v) for k, v in WNAMES.items()}
    lb_raw = din("hg_lb_raw", [4, 512])
    fin = din("final_norm", [D])
    rope = din("rope", [128, 2, L])
    amask_d = din("amask", [128, 4, 512])
    hmask_d = din("hmask", [64, 64])
    cm_d = din("cm", [128, 512])
    ident_d = din("ident", [128, 128])
    yT = nc.dram_tensor("yT", [D, L - 16], F32, kind="ExternalOutput").ap()
    G = {"hT": dsc("hT", [D, L], F32), "aT": dsc("aT", [DFF // 128, 128, L], BF16),
         "qT": dsc("qT", [NH, 96, L], BF16), "kT": dsc("kT", [NH, 64, L], BF16), "kpeT": dsc("kpeT", [32, L], BF16),
         "vTM": dsc("vTM", [L, 512], BF16), "hqT": dsc("hqT", [512, L], BF16), "hgT": dsc("hgT", [512, L], BF16),
         "logfT": dsc("logfT", [512, L], F32), "kkT": dsc("kkT", [512, L], BF16), "hiTM": dsc("hiTM", [L, 512], BF16),
         "gaT": dsc("gaT", [D, L], BF16), "gbT": dsc("gbT", [D, L], BF16),
         "oaT": dsc("oaT", [512, L], BF16), "orT": dsc("orT", [512, L], BF16),
         "lb_raw": lb_raw, "rope": rope}
    with contextlib.ExitStack() as st:
        cx = Ctx(nc, st)
        S = cx.S
        C = {}
        for (nm, src, shape) in (("amask", amask_d, [128, 4, 512]), ("hmask", hmask_d, [64, 64]),
                                 ("ident", ident_d, [128, 128])):
            C[nm] = cx.tile(shape, BF16, nm, root=True)
            S.add("pool", lambda e: e.dma_start(out=C[nm].t[:], in_=src), w=[C[nm].res], dma=C[nm].res)
        C["cm"] = cx.tile([128, 512], F32, "cm", root=True)
        S.add("sp", lambda e: e.dma_start(out=C["cm"].t[:], in_=cm_d), w=[C["cm"].res], dma=C["cm"].res)
        dummy = Res("cp")
        for c in range(8):
            S.add("sp", lambda e: e.dma_start(out=G["hT"][c * 128:(c + 1) * 128, :], in_=xT[c * 128:(c + 1) * 128, :]), dma=dummy)
        for l in range(nl):
            Wl = {k: v[l] for k, v in Wd.items()}
            ffn_block(cx, G["hT"], DRAMR, G["hT"], DRAMR, Wl["ffn1_norm"], Wl["ffn1_w_gu"], Wl["ffn1_w_down"], G["aT"], DRAMR, L)
            for th0 in (0, TH):
                mix_pre(cx, l, Wl, G, th0)
            mixers(cx, G, C, Wl)
            for th0 in (0, TH):
                mix_post(cx, Wl, G, th0)
            ffn_block(cx, G["hT"], DRAMR, G["hT"], DRAMR, Wl["ffn2_norm"], Wl["ffn2_w_gu"], Wl["ffn2_w_down"], G["aT"], DRAMR, L)
        for th0 in (0, TH):
            final_norm(cx, G, fin, yT, th0)
        S.finish()
        build_program.stats = (S.nops, S.nsem)
    return nc


def host_consts():
    L = L_TOT
    half = 16
    inv = 10000.0 ** (-np.arange(half, dtype=np.float32) / half)
    ang = np.arange(L, dtype=np.float32)[:, None] * inv[None, :]
    cos, sin = np.cos(ang).T, np.sin(ang).T
    rope = np.zeros((128, 2, L), np.float32)
    for p in range(128):
        r = p % 32
        rope[p, 0] = cos[r % 16]
        rope[p, 1] = -sin[r % 16] if r < 16 else sin[r % 16]
    kk = np.arange(128)[:, None, None]
    dd = np.arange(4)[None, :, None]
    qq = np.arange(512)[None, None, :]
    amask = (128 * dd + kk <= qq).astype(np.float32)
    hmask = (np.arange(64)[:, None] <= np.arange(64)[None, :]).astype(np.float32)
    cm = np.ones((128, 512), np.float32)
    cm[:, ::64] = 0.0
    return {"rope": rope, "amask": np.ascontiguousarray(amask), "hmask": hmask, "cm": cm,
            "ident": np.eye(128, dtype=np.float32)}


def host_weights(inp, nl):
    W = {}
    for k in WNAMES:
        if k in ("w_in", "w_uq", "w_ukv"):
            continue
        W[k] = np.ascontiguousarray(inp[k][:nl])
    w_in = inp["w_in"][:nl]
    ksw = np.concatenate([w_in[:, :, O_KPE + 16:O_KPE + 32], w_in[:, :, O_KPE:O_KPE + 16]], axis=-1)
    W["w_in"] = np.ascontiguousarray(np.concatenate([w_in, ksw], axis=-1))
    wq = inp["w_uq"][:nl].reshape(nl, 384, NH, 96)
    nope = wq[..., :64].reshape(nl, 384, 512)
    rp = wq[..., 64:]
    rsw = np.concatenate([rp[..., 16:], rp[..., :16]], axis=-1)
    W["w_uq"] = np.ascontiguousarray(np.concatenate([nope, rp.reshape(nl, 384, 256), rsw.reshape(nl, 384, 256)], axis=-1))
    wk = inp["w_ukv"][:nl].reshape(nl, 256, NH, 128)
    W["w_ukv"] = np.ascontiguousarray(np.concatenate([wk[..., :64].reshape(nl, 256, 512), wk[..., 64:].reshape(nl, 256, 512)], axis=-1))
    return W


_PROG = {}


def run_model(inp, nl=4, batches=(0, 1, 2, 3), trace=False):
    if nl not in _PROG:
        _PROG[nl] = build_program(nl)
    nc = _PROG[nl]
    W = host_weights(inp, nl)
    cst = host_consts()
    meta = np.asarray(inp["meta_tokens"], np.float32)
    maps = []
    for b in batches:
        h0 = np.concatenate([meta, np.asarray(inp["x"][b], np.float32)], axis=0)
        m = {"xT": np.ascontiguousarray(h0.T), "hg_lb_raw": np.asarray(inp["hg_lb_raw"], np.float32),
             "final_norm": np.asarray(inp["final_norm"], np.float32)}
        m.update(W)
        m.update(cst)
        maps.append(m)
    res = run_bass_kernel_spmd(nc, maps, core_ids=list(range(len(batches))), trace=trace)
    out = np.stack([np.ascontiguousarray(r["yT"].T) for r in res.results], axis=0)
    return out.astype(np.float32), res


def kernel(**inputs):
    out, _ = run_model(inputs, nl=4)
    return out
```
